# Optimizing a Trainium2 kernel written in Bass

```python
import math
import jax, jax.numpy as jnp
from jax import lax
import numpy as np

D_MODEL = 1024
BATCH = 2
SEQ = 8192
DEPTH = 1
DEC_BATCH = 32
DEC_SEQ = 4
PAST_LEN = 16384
PAGE_SIZE = 128

RET_HEADS = 4
RET_DK = 128
RET_DV = 256
RET_CHUNK = 128
ROPE_BASE = 10000.0
ATT_GROUPS = ((128, 1), (512, 4), (2048, 16))
N_GROUPS = 3
ATT_HEADS = 8
ATT_DH = 64
Q_BLOCK = 128
REL_BUCKETS = 32
REL_MAX_DIST = 2048
EPS = 1e-6

RET_QK_W = RET_HEADS * RET_DK
RET_V_W = RET_HEADS * RET_DV
ATT_W = N_GROUPS * ATT_HEADS * ATT_DH
ATT_OUT_W = ATT_HEADS * ATT_DH
IN_SPLITS = (RET_QK_W, RET_QK_W, RET_V_W, RET_V_W, ATT_W, ATT_W, ATT_W, ATT_OUT_W, D_MODEL, D_MODEL)
IN_W = 2 * RET_QK_W + 2 * RET_V_W + 3 * ATT_W + ATT_OUT_W + 2 * D_MODEL

kernel_name = 'retention_dilated_attn_hybrid_step'


def rms_norm(x, g):
    x32 = x.astype(jnp.float32)
    y = x32 * lax.rsqrt(jnp.mean(x32 * x32, axis=-1, keepdims=True) + EPS)
    return (y * g.astype(jnp.float32)).astype(x.dtype)


def head_rms(x, g):
    x32 = x.astype(jnp.float32)
    y = x32 * lax.rsqrt(jnp.mean(x32 * x32, axis=-1, keepdims=True) + EPS)
    return (y * g.astype(jnp.float32)).astype(x.dtype)


def head_group_norm(o, g):
    mu = jnp.mean(o, axis=-1, keepdims=True)
    var = jnp.mean(jnp.square(o - mu), axis=-1, keepdims=True)
    return (o - mu) * lax.rsqrt(var + EPS) * g.astype(jnp.float32).reshape(RET_HEADS, RET_DV)


def split_cols(p):
    out, s = [], 0
    for w in IN_SPLITS:
        out.append(p[..., s:s + w])
        s += w
    return out


def rotary(x, pos):
    half = x.shape[-1] // 2
    inv = ROPE_BASE ** (-jnp.arange(half, dtype=jnp.float32) / half)
    ang = pos[:, None] * inv[None, :]
    cos = jnp.cos(ang)[None, :, None, :]
    sin = jnp.sin(ang)[None, :, None, :]
    x32 = x.astype(jnp.float32)
    x1, x2 = x32[..., :half], x32[..., half:]
    return jnp.concatenate([x1 * cos - x2 * sin, x1 * sin + x2 * cos], axis=-1).astype(x.dtype)


def retention_scan(q, k, v, state0):
    B, T, H, _ = q.shape
    C = RET_CHUNK if T % RET_CHUNK == 0 else T
    n = T // C
    log_g = jnp.log1p(-(2.0 ** (-5.0 - jnp.arange(H, dtype=jnp.float32))))
    i = jnp.arange(C, dtype=jnp.float32)
    diff = i[:, None] - i[None, :]
    decay = jnp.where(diff[None] >= 0, jnp.exp(jnp.maximum(diff, 0.0)[None] * log_g[:, None, None]), 0.0)
    q_decay = jnp.exp((i + 1.0)[:, None] * log_g[None, :])
    k_decay = jnp.exp((C - 1.0 - i)[:, None] * log_g[None, :])
    chunk_decay = jnp.exp(C * log_g)

    def step(S, qkv):
        qc, kc, vc = qkv
        scores = jnp.einsum('bqhd,bkhd->bhqk', qc, kc) * decay[None]
        o = (jnp.einsum('bhqk,bkhe->bqhe', scores, vc)
             + jnp.einsum('bqhd,bhde->bqhe', qc, S) * q_decay[None, :, :, None])
        S = S * chunk_decay[None, :, None, None] + jnp.einsum('bkhd,bkhe->bhde', kc * k_decay[None, :, :, None], vc)
        return S, o

    xs = tuple(a.astype(jnp.float32).reshape(B, n, C, H, a.shape[-1]).transpose(1, 0, 2, 3, 4) for a in (q, k, v))
    S, o = lax.scan(step, state0.astype(jnp.float32), xs)
    o = o.transpose(1, 0, 2, 3, 4).reshape(B, T, H, RET_DV)
    return o, S


def t5_bucket(dist):
    max_exact = REL_BUCKETS // 2
    d = jnp.maximum(dist.astype(jnp.float32), 1.0)
    large = max_exact + (jnp.log(d / max_exact) / math.log(REL_MAX_DIST / max_exact)
                         * (REL_BUCKETS - max_exact)).astype(jnp.int32)
    large = jnp.minimum(large, REL_BUCKETS - 1)
    return jnp.where(dist < max_exact, dist, large)


def group_bias(rel_bias, g, win, dil):
    dist = dil * jnp.arange(win // dil + 1, dtype=jnp.int32)
    b = rel_bias[t5_bucket(dist)][:, g * ATT_HEADS:(g + 1) * ATT_HEADS]
    return b.astype(jnp.float32).T


def dilated_group(q, kv_ext, start, dil, bias):
    tq = q.shape[1]
    M = bias.shape[1]
    idx = start + jnp.arange(tq)[:, None] - dil * jnp.arange(M)[None, :]
    valid = idx >= 0
    kv = jnp.take(kv_ext, jnp.maximum(idx, 0), axis=1)
    logits = jnp.einsum('bqhd,bqmhd->bhqm', q, kv[:, :, :, 0]).astype(jnp.float32) + bias[None, :, None, :]
    logits = jnp.where(valid[None, None], logits, -jnp.inf)
    lse = jax.nn.logsumexp(logits, axis=-1)
    p = jnp.exp(logits - lse[..., None]).astype(kv.dtype)
    o = jnp.einsum('bhqm,bqmhd->bqhd', p, kv[:, :, :, 1])
    return o, lse.transpose(0, 2, 1)


def dilated_attention(q, kv_exts, buf_lens, rel_bias):
    B, T = q.shape[:2]
    biases = [group_bias(rel_bias, g, w, d) for g, (w, d) in enumerate(ATT_GROUPS)]

    def attend(q_blk, t0):
        outs, lses = [], []
        for g, (win, dil) in enumerate(ATT_GROUPS):
            o, lse = dilated_group(q_blk[:, :, g], kv_exts[g], buf_lens[g] + t0, dil, biases[g])
            outs.append(o)
            lses.append(lse)
        w = jax.nn.softmax(jnp.stack(lses, axis=2), axis=2)
        o = jnp.einsum('bqgh,bqghd->bqhd', w, jnp.stack(outs, axis=2).astype(jnp.float32))
        return o.astype(q_blk.dtype)

    if T > Q_BLOCK and T % Q_BLOCK == 0:
        nb = T // Q_BLOCK
        qb = q.reshape(B, nb, Q_BLOCK, N_GROUPS, ATT_HEADS, ATT_DH).transpose(1, 0, 2, 3, 4, 5)
        out = lax.map(lambda a: attend(a[0], a[1]), (qb, jnp.arange(nb, dtype=jnp.int32) * Q_BLOCK))
        return out.transpose(1, 0, 2, 3, 4).reshape(B, T, ATT_HEADS, ATT_DH)
    return attend(q, 0)


def mixer_layer(x, pos0, ret_state0, kv_bufs, rel_bias, w_norm, w_in, q_norm, k_norm,
                ret_norm, w_proj_ret, w_proj_att, w_out):
    B, T, _ = x.shape
    h = rms_norm(x, w_norm)
    rq, rk, rv, rg, aq, ak, av, ag, ga, gb = split_cols(h @ w_in)
    pos = pos0 + jnp.arange(T, dtype=jnp.float32)
    rq = rotary(rq.reshape(B, T, RET_HEADS, RET_DK), pos)
    rk = rotary(rk.reshape(B, T, RET_HEADS, RET_DK), pos) * (RET_DK ** -0.5)
    rv = rv.reshape(B, T, RET_HEADS, RET_DV)
    ret_o, ret_state = retention_scan(rq, rk, rv, ret_state0)
    ret_o = head_group_norm(ret_o, ret_norm).reshape(B, T, RET_V_W).astype(x.dtype)
    o_a = (jax.nn.silu(rg) * ret_o) @ w_proj_ret
    aq = head_rms(aq.reshape(B, T, N_GROUPS, ATT_HEADS, ATT_DH), q_norm) * (ATT_DH ** -0.5)
    ak = head_rms(ak.reshape(B, T, N_GROUPS, ATT_HEADS, ATT_DH), k_norm)
    av = av.reshape(B, T, N_GROUPS, ATT_HEADS, ATT_DH)
    kv_new = jnp.stack([ak, av], axis=3)
    kv_exts, buf_lens, kv_rows = [], [], []
    for g, (win, dil) in enumerate(ATT_GROUPS):
        new_g = kv_new[:, :, g]
        if kv_bufs is None:
            kv_exts.append(new_g)
            buf_lens.append(0)
            kv_rows.append(new_g[:, T - min(win, T):])
        else:
            buf = kv_bufs[g].astype(new_g.dtype)
            kv_exts.append(jnp.concatenate([buf, new_g], axis=1))
            buf_lens.append(buf.shape[1])
            kv_rows.append(new_g)
    att_o = dilated_attention(aq, kv_exts, buf_lens, rel_bias).reshape(B, T, ATT_OUT_W)
    o_b = (jax.nn.silu(ag) * att_o) @ w_proj_att
    merged = jax.nn.sigmoid(ga) * o_a + jax.nn.sigmoid(gb) * o_b
    return x + merged @ w_out, ret_state, kv_rows


def setup_inputs(seed: int = 0) -> dict:
    key = jax.random.key(seed)
    ks = jax.random.split(key, 16)
    f32 = jnp.float32
    nrm = lambda k, s: jax.random.normal(k, s, f32)
    caches = [nrm(ks[2 + g], (DEPTH, DEC_BATCH, min(w, PAST_LEN), 2, ATT_HEADS, ATT_DH))
              for g, (w, d) in enumerate(ATT_GROUPS)]
    return {
        'x_prompt': nrm(ks[0], (BATCH, SEQ, D_MODEL)),
        'x_sample': nrm(ks[1], (DEC_BATCH, DEC_SEQ, D_MODEL)),
        'cache_kv_w128': caches[0],
        'cache_kv_w512': caches[1],
        'cache_kv_w2048': caches[2],
        'state_retention': 0.5 * nrm(ks[5], (DEPTH, DEC_BATCH, RET_HEADS, RET_DK, RET_DV)),
        'w_norm': 1.0 + 0.02 * nrm(ks[6], (DEPTH, D_MODEL)),
        'w_in': nrm(ks[7], (DEPTH, D_MODEL, IN_W)) * D_MODEL ** -0.5,
        'q_norm': 1.0 + 0.02 * nrm(ks[8], (DEPTH, ATT_DH)),
        'k_norm': 1.0 + 0.02 * nrm(ks[9], (DEPTH, ATT_DH)),
        'rel_bias': 0.5 * nrm(ks[10], (REL_BUCKETS, N_GROUPS * ATT_HEADS)),
        'ret_norm': 1.0 + 0.02 * nrm(ks[11], (DEPTH, RET_V_W)),
        'w_proj_ret': nrm(ks[12], (DEPTH, RET_V_W, D_MODEL)) * RET_V_W ** -0.5,
        'w_proj_att': nrm(ks[13], (DEPTH, ATT_OUT_W, D_MODEL)) * ATT_OUT_W ** -0.5,
        'w_out': nrm(ks[14], (DEPTH, D_MODEL, D_MODEL)) * D_MODEL ** -0.5,
    }


def reference(x_prompt, x_sample, cache_kv_w128, cache_kv_w512, cache_kv_w2048, state_retention,
              w_norm, w_in, q_norm, k_norm, rel_bias, ret_norm, w_proj_ret, w_proj_att, w_out):
    y_p, y_s = x_prompt, x_sample
    rs_p, rs_s = [], []
    kvp = [[], [], []]
    kvs = [[], [], []]
    for l in range(DEPTH):
        lw = (w_norm[l], w_in[l], q_norm[l], k_norm[l], ret_norm[l], w_proj_ret[l], w_proj_att[l], w_out[l])
        zero_state = jnp.zeros((x_prompt.shape[0], RET_HEADS, RET_DK, RET_DV), jnp.float32)
        y_p, st_p, rows_p = mixer_layer(y_p, 0, zero_state, None, rel_bias, *lw)
        bufs = (cache_kv_w128[l], cache_kv_w512[l], cache_kv_w2048[l])
        y_s, st_s, rows_s = mixer_layer(y_s, PAST_LEN, state_retention[l], bufs, rel_bias, *lw)
        rs_p.append(st_p)
        rs_s.append(st_s)
        for g in range(N_GROUPS):
            kvp[g].append(rows_p[g])
            kvs[g].append(rows_s[g])
    ret_state_prompt = jnp.stack(rs_p)
    ret_state_sample = jnp.stack(rs_s)
    kv_w128_prompt = jnp.stack(kvp[0])
    kv_w512_prompt = jnp.stack(kvp[1])
    kv_w2048_prompt = jnp.stack(kvp[2])
    kv_w128_sample = jnp.stack(kvs[0])
    kv_w512_sample = jnp.stack(kvs[1])
    kv_w2048_sample = jnp.stack(kvs[2])
    return (y_p, y_s, ret_state_prompt, ret_state_sample, kv_w128_prompt, kv_w512_prompt, kv_w2048_prompt, kv_w128_sample, kv_w512_sample, kv_w2048_sample)
```

```python
import os
import numpy as np
from contextlib import ExitStack
import concourse.bass as bass
import concourse.mybir as mybir
from concourse.bass_utils import run_bass_kernel_spmd

F32 = mybir.dt.float32
BF16 = mybir.dt.bfloat16
ALU = mybir.AluOpType
AF = mybir.ActivationFunctionType
AX = mybir.AxisListType

NCORES = 8
D = 1024
SEQ = 8192
OWN = 2048
INW = 10240
EPS = 1e-6
NDMA = 40
STAGE = int(os.environ.get("MK_STAGE", "9"))
DBG = os.environ.get("MK_DBG", "")
QST = os.environ.get("MK_QST", "sp")


class Ev:
    __slots__ = ("key", "seq", "sem", "val", "front")

    def __init__(self, key, seq, sem, val, front=None):
        self.key, self.seq, self.sem, self.val = key, seq, sem, val
        self.front = front or {}


class Buf:
    __slots__ = ("w", "r", "name", "excl")

    def __init__(self, name="", excl=False):
        self.w = None
        self.r = {}
        self.name = name
        self.excl = excl


class Eng:
    def __init__(self, name, h, sem, inorder=False):
        self.name, self.h, self.sem, self.inorder = name, h, sem, inorder
        self.count = 0
        self.waited = {}


class Tracker:
    def __init__(self, nc, es):
        self.nc = nc
        self.e = {}
        for name, h, io in [("pe", nc.tensor, True), ("act", nc.scalar, False), ("dve", nc.vector, False),
                            ("pool", nc.gpsimd, False), ("sp", nc.sync, False)]:
            self.e[name] = Eng(name, h, es.enter_context(nc.semaphore("s_" + name)), io)
        self.dsem = [es.enter_context(nc.semaphore("d%d" % i)) for i in range(NDMA)]
        self.duse = [0] * NDMA
        self.di = 0
        self.dma_evs = []
        self.n_wait = 0

    def _wait(self, e, ev):
        if ev is None:
            return
        if ev.key == e.name and e.inorder:
            return
        if e.waited.get(ev.key, 0) >= ev.seq:
            return
        e.h.wait_ge(ev.sem, ev.val)
        self.n_wait += 1
        e.nw = getattr(e, "nw", 0) + 1
        e.waited[ev.key] = ev.seq
        for k, v in ev.front.items():
            if e.waited.get(k, 0) < v:
                e.waited[k] = v

    def _deps(self, e, reads, writes):
        for b in reads:
            self._wait(e, b.w)
            if b.excl:
                for ev in list(b.r.values()):
                    if ev.key != e.name:
                        self._wait(e, ev)
        for b in writes:
            self._wait(e, b.w)
            for ev in list(b.r.values()):
                self._wait(e, ev)

    def op(self, eng, fn, reads=(), writes=()):
        e = self.e[eng]
        self._deps(e, reads, writes)
        inst = fn(e.h)
        e.count += 1
        inst.then_inc(e.sem, 1)
        ev = Ev(e.name, e.count, e.sem, e.count, dict(e.waited))
        for b in reads:
            b.r[e.name] = ev
        for b in writes:
            b.w = ev
            b.r = {}
        return ev

    def dma(self, q, fn, reads=(), writes=()):
        e = self.e[q]
        self._deps(e, reads, writes)
        slot = self.di % NDMA
        self.di += 1
        if self.duse[slot] > 0:
            self._wait(e, Ev(("dma", slot), self.duse[slot], self.dsem[slot], 16 * self.duse[slot]))
        inst = fn(e.h)
        self.duse[slot] += 1
        inst.then_inc(self.dsem[slot], 16)
        ev = Ev(("dma", slot), self.duse[slot], self.dsem[slot], 16 * self.duse[slot], dict(e.waited))
        for b in reads:
            b.r[ev.key] = ev
        for b in writes:
            b.w = ev
            b.r = {}
        self.dma_evs.append(ev)
        return ev

    def barrier(self):
        evs = [Ev(x.name, x.count, x.sem, x.count) for x in self.e.values() if x.count > 0]
        devs = [Ev(("dma", s), self.duse[s], self.dsem[s], 16 * self.duse[s]) for s in range(NDMA) if self.duse[s] > 0]
        for e in self.e.values():
            for ev in evs + devs:
                if ev.key != e.name:
                    self._wait(e, ev)
                elif not e.inorder:
                    self._wait(e, ev)

    def finish(self):
        e = self.e["sp"]
        for s in range(NDMA):
            if self.duse[s] > 0:
                self._wait(e, Ev(("dma", s), self.duse[s], self.dsem[s], 16 * self.duse[s]))
        for x in self.e.values():
            if x.name != "sp" and x.count > 0:
                self._wait(e, Ev(x.name, x.count, x.sem, x.count))


class SB:
    def __init__(self, h, F):
        self.h, self.F = h, F

    def ap(self, off, dims, p0=0, n=128):
        return bass.AP(self.h, p0 * self.F + off, [[self.F, n]] + [list(d) for d in dims])

    def s(self, off, cnt, p0=0, n=128):
        return bass.AP(self.h, p0 * self.F + off, [[self.F, n], [1, cnt]])


def dap(h, off, dims):
    return bass.AP(h, off, [list(d) for d in dims])


def build_program():
    nc = bass.Bass("TRN2", target_bir_lowering=False)
    es = ExitStack()

    def din(name, shape, dt=F32):
        return nc.dram_tensor(name, list(shape), dt, kind="ExternalInput")

    def dout(name, shape, dt=F32):
        return nc.dram_tensor(name, list(shape), dt, kind="ExternalOutput")

    def dscr(name, shape, dt):
        return nc.dram_tensor(name, list(shape), dt)

    x_parts = [din("x_h", [OWN, D]), din("x_o", [OWN, D])]
    w_in_d = din("w_in", [D, INW])
    wpr_d = din("w_proj_ret", [1024, 1024])
    wpa_d = din("w_proj_att", [512, 1024])
    wo_d = din("w_out", [1024, 1024])
    gT_d = din("gT", [128, 8])
    rnT_d = din("rnT", [128, 8])
    gq_d = din("gq_tab", [128, 64])
    gk_d = din("gk_tab", [128, 64])
    bt_d = din("bt", [128, 24 * 256])
    tmask_d = din("tmask", [128, 256])
    cos_d = din("cos", [4 * OWN, 64])
    sin_d = din("sin", [4 * OWN, 64])
    dec_d = din("dec", [128, 16])
    cmask_d = din("cmask", [128, 128])
    ident_d = din("ident", [128, 128])
    hv_d = din("hv", [128, 64])
    xpp_d = din("x_pp", [3 * OWN, D])
    xs_d = din("x_s", [128, D])
    coss_d = din("cos_s", [128, 64])
    sins_d = din("sin_s", [128, 64])
    decs_d = din("dec_s", [128, 16])
    smask_d = din("smask", [128, 128])
    colmask_d = din("colmask", [128, 512])
    bts_d = din("bts", [128, 432])
    vms_d = din("vms", [128, 432])
    cch_d = [din("c128", [4 * 128, 2, 512]), din("c512", [4 * 512, 2, 512]), din("c2048", [4 * 2048, 2, 512])]
    s0_d = din("s0", [16, 128, 256])

    y_d = dout("y", [OWN, D])
    sfin_d = dout("sfin", [512, 256])
    kvs_d = [dout("kvs%d" % g, [128, 2, 512]) for g in range(3)]
    rss_d = dout("rss", [16, 128, 256])
    ys_d = dout("ys", [16, D])
    kv128_d = dout("kv128", [128, 2, 512])
    kv512_d = dout("kv512", [512, 2, 512])
    kv2048_d = dout("kv2048", [2048, 2, 512])

    wbf_d = dscr("wbf", [20, 128, 8 * 512], BF16)
    wprb_d = dscr("wprb", [2, 128, 8 * 512], BF16)
    wpab_d = dscr("wpab", [2, 128, 4 * 512], BF16)
    wob_d = dscr("wob", [2, 128, 8 * 512], BF16)
    vscr1_d = dscr("vscr1", [2, 512, 512], BF16)
    vscr2_d = dscr("vscr2", [2, 2048, 512], BF16)
    att_scr_d = dscr("att_scr", [16, 512], F32)

    T = Tracker(nc, es)
    sbuf_used = [0]

    def sb(name, F, dt=F32):
        h = es.enter_context(nc.sbuf_tensor("sb_" + name, [128, F], dt))
        sbuf_used[0] += F * (4 if dt == F32 else 2)
        return SB(h, F)

    def ps(name, F, dt=F32):
        return SB(es.enter_context(nc.psum_tensor(name, [128, F], dt)), F)

    ps_mm = [ps("ps_mm0", 512), ps("ps_mm1", 512)]
    ps_tr = ps("ps_tr", 1024, BF16)
    ps_st = ps("ps_st", 512)
    ps_num = ps("ps_num", 512)
    ps_den = ps("ps_den", 512)
    ps_ret = ps("ps_ret", 512)
    ps_misc = ps("ps_misc", 512)
    B_mm = [Buf("mm0", True), Buf("mm1", True)]
    B_tr = Buf("tr", True)
    _bst = Buf("st", True)
    B_st = [_bst, _bst]
    _bret = Buf("ret", True)
    B_num, B_den, B_ret_sc, B_ret_o, B_misc = Buf("num", True), Buf("den", True), _bret, _bret, Buf("misc", True)
    ps_trs = [ps_tr, SB(ps_misc.h.bitcast(BF16), 1024)]
    B_trs = [B_tr, B_misc]
    ps_mm += [ps_st, ps_num, ps_den]
    B_mm += [B_st[0], B_num, B_den]

    ident_f = sb("ident_f", 128)
    ident_b = sb("ident_b", 128, BF16)
    cmask = sb("cmask", 128)
    gq_tab = sb("gq_tab", 64)
    gk_tab = sb("gk_tab", 64)
    dec = sb("dec", 16)
    gT = sb("gT", 8)
    rnT = sb("rnT", 8)
    hv_f = sb("hv_f", 64)
    hvb = sb("hvb", 64, BF16)
    ones64 = sb("ones64", 64, BF16)
    EB = sb("EB", 24 * 256, BF16)
    B_const = Buf("const")
    B_EB = Buf("EB")

    mult, add, sub = ALU.mult, ALU.add, ALU.subtract

    def ld(dst, src_h, F, q="sp", buf=B_const):
        T.dma(q, lambda e: e.dma_start(out=dst.s(0, F), in_=dap(src_h, 0, [[F, 128], [1, F]])), writes=[buf])

    ld(gT, gT_d, 8)
    ld(rnT, rnT_d, 8)
    with ExitStack() as es0:
        wst = [SB(es0.enter_context(nc.sbuf_tensor("wst%d" % i, [128, 8 * 512], F32)), 4096) for i in range(2)]
        wcv = [SB(es0.enter_context(nc.sbuf_tensor("wcv%d" % i, [128, 8 * 512], BF16)), 4096) for i in range(2)]
        B_wst = [Buf(), Buf()]
        B_wcv = [Buf(), Buf()]
        B_wscr = Buf("wscr")
        jobs = []
        for c in range(20):
            jobs.append((w_in_d, INW, c * 512, 8, gT, wbf_d, c))
        for c in range(2):
            jobs.append((wpr_d, 1024, c * 512, 8, rnT, wprb_d, c))
        for c in range(2):
            jobs.append((wpa_d, 1024, c * 512, 4, None, wpab_d, c))
        for c in range(2):
            jobs.append((wo_d, 1024, c * 512, 8, None, wob_d, c))
        engs = ["dve", "act"]
        ei = 0
        def cv_load(ji):
            src, rowlen, col0, nkt, sc, dst, c = jobs[ji]
            b = ji % 2
            T.dma("sp", lambda e: e.dma_start(
                out=wst[b].ap(0, [[512, nkt], [1, 512]]),
                in_=dap(src, col0, [[rowlen, 128], [128 * rowlen, nkt], [1, 512]])), writes=[B_wst[b]])
        cv_load(0)
        for ji, (src, rowlen, col0, nkt, sc, dst, c) in enumerate(jobs):
            b = ji % 2
            if ji + 1 < len(jobs):
                cv_load(ji + 1)
            for kt in range(nkt):
                en = engs[ei % 2]
                ei += 1
                o_ap = wcv[b].s(kt * 512, 512)
                i_ap = wst[b].s(kt * 512, 512)
                if sc is None:
                    if en == "act":
                        T.op(en, lambda e, o_ap=o_ap, i_ap=i_ap: e.activation(out=o_ap, in_=i_ap, func=AF.Copy),
                             reads=[B_wst[b]], writes=[B_wcv[b]])
                    else:
                        T.op(en, lambda e, o_ap=o_ap, i_ap=i_ap: e.tensor_copy(out=o_ap, in_=i_ap), reads=[B_wst[b]], writes=[B_wcv[b]])
                else:
                    s_ap = sc.s(kt, 1)
                    if en == "act":
                        T.op(en, lambda e, o_ap=o_ap, i_ap=i_ap, s_ap=s_ap: e.activation(out=o_ap, in_=i_ap, func=AF.Copy, scale=s_ap),
                             reads=[B_wst[b], B_const], writes=[B_wcv[b]])
                    else:
                        T.op(en, lambda e, o_ap=o_ap, i_ap=i_ap, s_ap=s_ap: e.tensor_scalar(out=o_ap, in0=i_ap, scalar1=s_ap, scalar2=None, op0=mult),
                             reads=[B_wst[b], B_const], writes=[B_wcv[b]])
            T.dma("sp", lambda e, b=b, dst=dst, c=c, nkt=nkt: e.dma_start(
                out=dap(dst, c * 128 * nkt * 512, [[nkt * 512, 128], [1, nkt * 512]]), in_=wcv[b].s(0, nkt * 512)),
                reads=[B_wcv[b]], writes=[B_wscr])
        T.barrier()
    B_wscr = Buf("wscr_done")

    wch = [sb("wch%d" % i, 8 * 512, BF16) for i in range(2)]
    B_wch = [Buf("wch%d" % i) for i in range(2)]
    xin = [sb("xin%d" % i, 1024) for i in range(2)]
    B_xin = [Buf("xin0"), Buf("xin1")]
    xb = sb("xb", 1024, BF16)
    B_xb = Buf("xb")
    xT = sb("xT", 8 * 512, BF16)
    B_xT = Buf("xT")
    rstd = sb("rstd", 4)
    B_rstd = Buf("rstd")
    cosw = sb("cosw", 256)
    sinw = sb("sinw", 256)
    B_cs = Buf("cs")

    kT0 = sb("kT0", 4 * 640, BF16)
    V0 = sb("V0", 5 * 512, BF16)
    kT1 = sb("kT1", 4 * 1024, BF16)
    V1 = sb("V1", 2 * 4 * 512, BF16)
    kT2 = sb("kT2", 4 * 4096, BF16)
    V2s = [sb("V2s0", 4096, BF16), sb("V2s1", 4096, BF16)]
    B_kT0 = [Buf("kT0_%d" % i) for i in range(5)]
    B_V0 = [Buf("V0_%d" % i) for i in range(5)]
    B_kT1 = [Buf("kT1_0"), Buf("kT1_1")]
    B_V1 = [Buf("V1_0"), Buf("V1_1")]
    B_kT2 = [Buf("kT2_%d" % i) for i in range(8)]
    B_V2s = [Buf("V2s0"), Buf("V2s1")]
    B_vscr1 = [Buf("vscr1_0"), Buf("vscr1_1")]
    B_vscr2 = [Buf("vscr2_%d" % i) for i in range(8)]
    B_rstd_d = Buf("rstd_d")

    S = sb("S", 1024)
    Sbf = sb("Sbf", 1024, BF16)
    B_S = [Buf("S%d" % h) for h in range(4)]
    B_Sbf = [Buf("Sbf%d" % h) for h in range(4)]
    A = sb("A", 4 * 1024, BF16)
    B_A = Buf("A")
    sigb = sb("sigb", 4 * 512, BF16)
    B_sig = Buf("sig")

    U1 = sb("U1", 14336, BF16)
    B_seg = [Buf("seg%d" % i) for i in range(5)]
    O_qT, O_kTr, O_ktok, O_rv, O_on = 0, 2048, 4096, 6144, 10240
    O_gretT = 0
    O_aqT, O_sagT, O_gatT, O_mrg, O_mT = 0, 6144, 8192, 10240, 0

    sq = sb("sq", 1024)
    B_sqh = [Buf("sq0"), Buf("sq1")]
    B_sq = B_sqh[0]
    B_smh = [Buf("smh0"), Buf("smh1")]
    kf = [sb("kf0", 512), sb("kf1", 512)]
    B_kf = [Buf("kf0"), Buf("kf1")]
    kb = [sb("kb%d" % i, 512, BF16) for i in range(6)]
    B_kb = [Buf("kb%d" % i) for i in range(6)]
    small = sb("small", 64)
    B_small = Buf("small")
    smh = sb("smh", 16)
    epsc = sb("epsc", 1)
    pexp = [sb("pexp0", 256, BF16), sb("pexp1", 256, BF16)]
    B_pexp = [Buf("pexp0"), Buf("pexp1")]
    vst = [sb("vst0", 512, BF16), sb("vst1", 512, BF16)]
    B_vst = [Buf("vst0"), Buf("vst1")]
    of32 = [sb("of0", 512), sb("of1", 512)]
    B_of = [Buf("of0"), Buf("of1")]
    Dbf = [sb("Dbf0", 128, BF16), sb("Dbf1", 128, BF16)]
    B_D = [Buf("D0"), Buf("D1")]
    rot = sb("rot", 1024)
    B_rot = Buf("rot")
    rot2 = sb("rot2", 512)
    B_rot2 = Buf("rot2")
    gst, B_gst = rot, B_rot
    Rb, B_Rb = kf[0], B_kf[0]
    tmpa, B_tmpa = kf[1], B_kf[1]

    ctr = {"mm": 0, "kf": 0, "kb": 0, "vst": 0, "of": 0, "D": 0, "st": 0, "pexp": 0, "xin": 0, "v2s": 0, "hr": 0, "mm5": 0, "trb": 0}

    def nxt(k, n=2):
        v = ctr[k] % n
        ctr[k] += 1
        return v


    ld(ident_f, ident_d, 128)
    ld(cmask, cmask_d, 128)
    ld(gq_tab, gq_d, 64)
    ld(gk_tab, gk_d, 64)
    ld(dec, dec_d, 16)
    ld(hv_f, hv_d, 64)
    T.op("dve", lambda e: e.tensor_copy(out=ident_b.s(0, 128), in_=ident_f.s(0, 128)), reads=[B_const], writes=[B_const])
    T.op("dve", lambda e: e.tensor_copy(out=hvb.s(0, 64), in_=hv_f.s(0, 64)), reads=[B_const], writes=[B_const])
    T.op("dve", lambda e: e.memset(ones64.s(0, 64), 1.0), writes=[B_const])
    T.op("dve", lambda e: e.memset(epsc.s(0, 1), EPS), writes=[B_const])
    T.op("dve", lambda e: e.tensor_scalar(out=gq_tab.s(0, 64), in0=gq_tab.s(0, 64), scalar1=0.125, scalar2=None, op0=mult),
         reads=[B_const], writes=[B_const])
    T.op("pool", lambda e: e.memset(S.s(0, 1024), 0.0), writes=B_S)
    for tns, F_, bl in [(kT0, 2560, B_kT0), (V0, 2560, B_V0), (kT1, 4096, B_kT1), (V1, 4096, B_V1)]:
        T.op("pool", lambda e, tns=tns, F_=F_: e.memset(tns.s(0, F_), 0.0), writes=bl)
    for i in range(4):
        T.op("pool", lambda e, i=i: e.memset(kT2.s(i * 4096, 4096), 0.0), writes=B_kT2)
    for i in range(2):
        T.op("pool", lambda e, i=i: e.memset(V2s[i].s(0, 4096), 0.0), writes=[B_V2s[i]])

    tmk = sb("tmk", 256)
    ld(tmk, tmask_d, 256)
    for i in range(12):
        j = nxt("kf")
        T.dma("sp", lambda e, i=i, j=j: e.dma_start(out=kf[j].s(0, 512), in_=dap(bt_d, i * 512, [[24 * 256, 128], [1, 512]])),
              writes=[B_kf[j]])
        T.op("act", lambda e, j=j: e.activation(out=kf[j].s(0, 512), in_=kf[j].s(0, 512), func=AF.Exp),
             reads=[B_kf[j]], writes=[B_kf[j]])
        T.op("dve", lambda e, i=i, j=j: e.tensor_tensor(out=EB.ap(i * 512, [[256, 2], [1, 256]]), in0=kf[j].ap(0, [[256, 2], [1, 256]]),
                                                     in1=tmk.ap(0, [[0, 2], [1, 256]]), op=mult),
             reads=[B_kf[j], B_const], writes=[B_EB])

    def window_prep(row0, widx_scr, cs_row0=None, src=None, ntiles=4):
        for t in range(ntiles):
            j = nxt("xin")
            src_h = x_parts[row0 // OWN] if src is None else src
            T.dma("sp", lambda e, j=j, t=t: e.dma_start(out=xin[j].s(0, 1024), in_=dap(src_h, ((row0 % OWN if src is None else row0) + t * 128) * D, [[D, 128], [1, D]])),
                  writes=[B_xin[j]])
            T.op("act", lambda e, j=j: e.activation(out=sq.s(0, 1024), in_=xin[j].s(0, 1024), func=AF.Square),
                 reads=[B_xin[j]], writes=B_sqh)
            T.op("dve", lambda e, t=t: e.reduce_sum(out=rstd.s(t, 1), in_=sq.s(0, 1024), axis=AX.X), reads=B_sqh, writes=[B_rstd])
            T.op("act", lambda e, j=j: e.activation(out=xb.s(0, 1024), in_=xin[j].s(0, 1024), func=AF.Copy),
                 reads=[B_xin[j]], writes=[B_xb])
            for kt in range(8):
                T.op("pe", lambda e, kt=kt: e.transpose(out=ps_tr.s(kt * 128, 128), in_=xb.s(kt * 128, 128), identity=ident_b.s(0, 128)),
                     reads=[B_xb, B_const], writes=[B_tr])
            T.op("dve", lambda e, t=t: e.tensor_copy(out=xT.ap(t * 128, [[512, 8], [1, 128]]), in_=ps_tr.ap(0, [[128, 8], [1, 128]])),
                 reads=[B_tr], writes=[B_xT])
        T.op("dve", lambda e: e.tensor_scalar(out=rstd.s(0, 4), in0=rstd.s(0, 4), scalar1=1.0 / D, scalar2=EPS, op0=mult, op1=add),
             reads=[B_rstd], writes=[B_rstd])
        T.op("act", lambda e: e.activation(out=rstd.s(0, 4), in_=rstd.s(0, 4), func=AF.Sqrt), reads=[B_rstd], writes=[B_rstd])
        T.op("dve", lambda e: e.reciprocal(out=rstd.s(0, 4), in_=rstd.s(0, 4)), reads=[B_rstd], writes=[B_rstd])
        if cs_row0 is not None:
            T.dma("sp", lambda e: e.dma_start(out=cosw.ap(0, [[64, 4], [1, 64]]), in_=dap(cos_d, cs_row0 * 64, [[64, 128], [128 * 64, 4], [1, 64]])),
                  writes=[B_cs])
            T.dma("sp", lambda e: e.dma_start(out=sinw.ap(0, [[64, 4], [1, 64]]), in_=dap(sin_d, cs_row0 * 64, [[64, 128], [128 * 64, 4], [1, 64]])),
                  writes=[B_cs])

    wl_state = {"i": 0}

    def run_loads(loads, body):
        n = len(loads)
        base = wl_state["i"]

        def issue(i):
            h, c, nkt = loads[i]
            b = (base + i) % 2
            T.dma("sp", lambda e: e.dma_start(out=wch[b].s(0, nkt * 512), in_=dap(h, c * 128 * nkt * 512, [[nkt * 512, 128], [1, nkt * 512]])),
                  reads=[B_wscr], writes=[B_wch[b]])

        issue(0)
        for i in range(n):
            b = (base + i) % 2
            if i + 1 < n:
                issue(i + 1)
            body(i, wch[b], B_wch[b])
        wl_state["i"] = base + n
        flush_pending()

    pending = []
    tile_no = [0]

    def flush_pending(upto=None):
        while pending and (upto is None or pending[0][0] <= upto):
            pending.pop(0)[1]()

    def inproj_tile(wb, Bw, t, nkt=8, lhs=None, Blhs=None):
        if lhs is not None:
            flush_pending()
        m = nxt("mm5", 5)
        for kt in range(nkt):
            l_ap = xT.s(kt * 512 + t * 128, 128) if lhs is None else lhs(kt)
            T.op("pe", lambda e, kt=kt, l_ap=l_ap: e.matmul(ps_mm[m].s(0, 512), l_ap, wb.s(kt * 512, 512), start=(kt == 0), stop=(kt == nkt - 1)),
                 reads=([B_xT] if Blhs is None else Blhs) + [Bw], writes=[B_mm[m]])
        tile_no[0] += 1
        flush_pending(tile_no[0] - 3)
        return ps_mm[m], B_mm[m]

    def transposes_to(src, Bsrc, src_off, n, dst_ap, Bdst, eng="act", scale=None):
        def emit():
            jt = nxt("trb")
            pt_, Bt_ = ps_trs[jt], B_trs[jt]
            for j in range(n):
                T.op("pe", lambda e, j=j: e.transpose(out=pt_.s(j * 128, 128), in_=src.s(src_off + j * 128, 128), identity=ident_b.s(0, 128)),
                     reads=[Bsrc, B_const], writes=[Bt_])
            if eng == "act":
                T.op("act", lambda e: e.activation(out=dst_ap, in_=pt_.ap(0, [[128, n], [1, 128]]), func=AF.Copy), reads=[Bt_], writes=Bdst)
            else:
                T.op(eng, lambda e: e.tensor_copy(out=dst_ap, in_=pt_.ap(0, [[128, n], [1, 128]])), reads=[Bt_], writes=Bdst)
        pending.append((tile_no[0], emit))

    def head_rms(pm, Bpm, t, gtab, want_f32_out):
        hp = nxt("hr")
        so, co = hp * 512, 8 * hp
        Bq, Bs = B_sqh[hp], B_smh[hp]
        T.op("act", lambda e: e.activation(out=sq.s(so, 512), in_=pm.s(0, 512), func=AF.Square, scale=rstd.s(t, 1)),
             reads=[Bpm, B_rstd], writes=[Bq])
        T.op("dve", lambda e: e.reduce_sum(out=smh.s(co, 8), in_=sq.ap(so, [[64, 8], [1, 64]]), axis=AX.X), reads=[Bq], writes=[Bs])
        T.op("act", lambda e: e.activation(out=smh.s(co, 8), in_=smh.s(co, 8), func=AF.Sqrt, scale=1.0 / 64, bias=epsc.s(0, 1)),
             reads=[Bs, B_const], writes=[Bs])
        T.op("dve", lambda e: e.reciprocal(out=smh.s(co, 8), in_=smh.s(co, 8)), reads=[Bs], writes=[Bs])
        jf = nxt("kf")
        T.op("dve", lambda e: e.scalar_tensor_tensor(out=kf[jf].ap(0, [[64, 8], [1, 64]]), in0=pm.ap(0, [[64, 8], [1, 64]]), scalar=rstd.s(t, 1),
                                                     in1=smh.ap(co, [[1, 8], [0, 64]]), op0=mult, op1=mult),
             reads=[Bpm, Bs, B_rstd], writes=[B_kf[jf]])
        jb = nxt("kb", 6)
        T.op("dve", lambda e: e.tensor_tensor(out=kb[jb].ap(0, [[64, 8], [1, 64]]), in0=kf[jf].ap(0, [[64, 8], [1, 64]]), in1=gtab.ap(0, [[0, 8], [1, 64]]), op=mult),
             reads=[B_kf[jf], B_const], writes=[B_kb[jb]])
        jo = None
        if want_f32_out:
            jo = nxt("of")
            T.op("dve", lambda e: e.tensor_tensor(out=of32[jo].ap(0, [[64, 8], [1, 64]]), in0=kf[jf].ap(0, [[64, 8], [1, 64]]), in1=gtab.ap(0, [[0, 8], [1, 64]]), op=mult),
                 reads=[B_kf[jf], B_const], writes=[B_of[jo]])
        return jb, jo

    def v_epilogue(pm, Bpm, t, dst_ap, Bdst, want_f32_out):
        T.op("act", lambda e: e.activation(out=dst_ap, in_=pm.s(0, 512), func=AF.Copy, scale=rstd.s(t, 1)),
             reads=[Bpm, B_rstd], writes=Bdst)
        jo = None
        if want_f32_out:
            jo = nxt("of")
            T.op("dve", lambda e: e.tensor_scalar(out=of32[jo].s(0, 512), in0=pm.s(0, 512), scalar1=rstd.s(t, 1), scalar2=None, op0=mult),
                 reads=[Bpm, B_rstd], writes=[B_of[jo]])
        return jo

    def kv_store(jo, dst_h, row0, kvi, stride_rows=1):
        if "nokvst" in DBG:
            return
        T.dma(QST, lambda e: e.dma_start(out=dap(dst_h, (row0 * 2 + kvi) * 512, [[1024 * stride_rows, 128], [1, 512]]), in_=of32[jo].s(0, 512)),
              reads=[B_of[jo]], writes=[Buf()])

    def rotary(pm, Bpm, t, dec_off, dst_ap, Bdst, dec_sb=None):
        dec_sb = dec if dec_sb is None else dec_sb
        T.op("dve", lambda e: e.tensor_scalar(out=small.s(16, 4), in0=dec_sb.s(dec_off, 4), scalar1=rstd.s(t, 1), scalar2=None, op0=mult),
             reads=[B_const, B_rstd], writes=[B_small])
        c_ap = lambda nrep: cosw.ap(t * 64, [[0, nrep], [1, 64]])
        s_ap = lambda nrep: sinw.ap(t * 64, [[0, nrep], [1, 64]])
        T.op("dve", lambda e: e.tensor_tensor(out=rot.ap(0, [[64, 4], [1, 64]]), in0=pm.ap(64, [[128, 4], [1, 64]]), in1=s_ap(4), op=mult),
             reads=[Bpm, B_cs], writes=[B_rot])
        T.op("dve", lambda e: e.tensor_tensor(out=rot.ap(256, [[64, 4], [1, 64]]), in0=pm.ap(0, [[128, 4], [1, 64]]), in1=s_ap(4), op=mult),
             reads=[Bpm, B_cs], writes=[B_rot])
        T.op("dve", lambda e: e.tensor_tensor(out=rot.ap(512, [[64, 8], [1, 64]]), in0=pm.ap(0, [[64, 8], [1, 64]]), in1=c_ap(8), op=mult),
             reads=[Bpm, B_cs], writes=[B_rot])
        T.op("dve", lambda e: e.tensor_tensor(out=rot2.ap(0, [[128, 4], [1, 64]]), in0=rot.ap(512, [[128, 4], [1, 64]]),
                                               in1=rot.ap(0, [[64, 4], [1, 64]]), op=sub), reads=[B_rot], writes=[B_rot2])
        T.op("dve", lambda e: e.tensor_tensor(out=rot2.ap(64, [[128, 4], [1, 64]]), in0=rot.ap(512 + 64, [[128, 4], [1, 64]]),
                                               in1=rot.ap(256, [[64, 4], [1, 64]]), op=add), reads=[B_rot], writes=[B_rot2])
        T.op("dve", lambda e: e.tensor_tensor(out=dst_ap, in0=rot2.ap(0, [[128, 4], [1, 128]]), in1=small.ap(16, [[1, 4], [0, 128]]), op=mult),
             reads=[B_rot2, B_small], writes=Bdst)

    def state_update_pre(t, h, need_bf):
        T.op("pe", lambda e: e.matmul(ps_misc.s(0, 256), U1.s(O_ktok + t * 512 + h * 128, 128), U1.s(O_rv + t * 1024 + h * 256, 256),
                                      start=True, stop=True), reads=[B_seg[2], B_seg[3]], writes=[B_misc])
        T.op("dve", lambda e: e.scalar_tensor_tensor(out=S.s(h * 256, 256), in0=S.s(h * 256, 256), scalar=dec.s(8 + h, 1), in1=ps_misc.s(0, 256),
                                                     op0=mult, op1=add), reads=[B_misc, B_S[h], B_const], writes=[B_S[h]])
        if need_bf:
            T.op("act", lambda e: e.activation(out=Sbf.s(h * 256, 256), in_=S.s(h * 256, 256), func=AF.Copy), reads=[B_S[h]], writes=[B_Sbf[h]])

    def state_update(t, h, need_bf):
        T.op("pe", lambda e: e.matmul(ps_misc.s(0, 256), U1.s(O_ktok + t * 512 + h * 128, 128), U1.s(O_rv + t * 1024 + h * 256, 256),
                                      start=True, stop=True), reads=[B_seg[2], B_seg[3]], writes=[B_misc])
        T.op("dve", lambda e: e.tensor_scalar(out=S.s(h * 256, 256), in0=S.s(h * 256, 256), scalar1=dec.s(8 + h, 1), scalar2=None, op0=mult),
             reads=[B_S[h], B_const], writes=[B_S[h]])
        T.op("dve", lambda e: e.scalar_tensor_tensor(out=S.s(h * 256, 256), in0=ps_misc.s(0, 256), scalar=dec.s(8 + h, 1), in1=S.s(h * 256, 256),
                                                     op0=mult, op1=add), reads=[B_misc, B_S[h], B_const], writes=[B_S[h]])
        if need_bf:
            T.op("act", lambda e: e.activation(out=Sbf.s(h * 256, 256), in_=S.s(h * 256, 256), func=AF.Copy), reads=[B_S[h]], writes=[B_Sbf[h]])

    def rk_rv_body(i, wb, Bw):
        for t in range(4):
            pm, Bpm = inproj_tile(wb, Bw, t)
            if i == 0:
                rotary(pm, Bpm, t, 12, U1.ap(O_ktok + t * 512, [[128, 4], [1, 128]]), [B_seg[2]])
            else:
                v_epilogue(pm, Bpm, t, U1.s(O_rv + t * 1024 + (i - 1) * 512, 512), [B_seg[3]], False)

    if STAGE >= 2:
        for w in range(12):
            window_prep(w * 512, 0, cs_row0=w * 512, src=xpp_d)
            run_loads([(wbf_d, 1, 8), (wbf_d, 2, 8), (wbf_d, 3, 8)], rk_rv_body)
            for t in range(4):
                for h in range(4):
                    state_update_pre(t, h, w == 11 and t == 3)

    def k_chunk(g, gw, wb, Bw):
        own = gw >= 4
        for t in range(4):
            pm, Bpm = inproj_tile(wb, Bw, t)
            want = own and ((g == 2) or (g == 1 and gw == 7 and "nokv512" not in DBG) or (g == 0 and gw == 7 and t == 3 and "nokv128" not in DBG))
            jb, jo = head_rms(pm, Bpm, t, gk_tab, want)
            if g == 0:
                slot = (gw * 4 + t) % 5
                dst, Bd = kT0.ap(slot * 128, [[640, 4], [1, 128]]), [B_kT0[slot]]
            elif g == 1:
                dst, Bd = kT1.ap((gw % 2) * 512 + t * 128, [[1024, 4], [1, 128]]), [B_kT1[gw % 2]]
            else:
                gwx = 4 if "kt2fix" in DBG else gw
                dst, Bd = kT2.ap(gwx * 512 + t * 128, [[4096, 4], [1, 128]]), [B_kT2[gw]]
            transposes_to(kb[jb], B_kb[jb], 0, 4, dst, Bd)
            if want:
                if g == 2:
                    kv_store(jo, kv2048_d, (gw - 4) * 512 + t * 128, 0)
                elif g == 1:
                    kv_store(jo, kv512_d, t * 128, 0)
                else:
                    kv_store(jo, kv128_d, 0, 0)

    def v_chunk(g, gw, wb, Bw):
        own = gw >= 4
        for t in range(4):
            pm, Bpm = inproj_tile(wb, Bw, t)
            want = own and ((g == 2) or (g == 1 and gw == 7 and "nokv512" not in DBG) or (g == 0 and gw == 7 and t == 3 and "nokv128" not in DBG))
            if g == 0:
                slot = (gw * 4 + t) % 5
                jo = v_epilogue(pm, Bpm, t, V0.s(slot * 512, 512), [B_V0[slot]], want)
            else:
                jv = nxt("vst")
                jo = v_epilogue(pm, Bpm, t, vst[jv].s(0, 512), [B_vst[jv]], want)
                if g == 1:
                    T.dma(QST, lambda e, jv=jv, t=t: e.dma_start(out=dap(vscr1_d, ((gw % 2) * 512 + t * 128) * 512, [[512, 128], [1, 512]]),
                                                                    in_=vst[jv].s(0, 512)), reads=[B_vst[jv]], writes=[B_vscr1[gw % 2]])
                elif "novscr2" not in DBG:
                    T.dma(QST, lambda e, jv=jv, t=t: e.dma_start(out=dap(vscr2_d, (gw * 512 + t * 128) * 512, [[512, 128], [1, 512]]),
                                                                    in_=vst[jv].s(0, 512)), reads=[B_vst[jv]], writes=[B_vscr2[gw]])
            if want:
                if g == 2:
                    kv_store(jo, kv2048_d, (gw - 4) * 512 + t * 128, 1)
                elif g == 1:
                    kv_store(jo, kv512_d, t * 128, 1)
                else:
                    kv_store(jo, kv128_d, 0, 1)
        if g == 1:
            sl = gw % 2
            T.dma(QST, lambda e: e.dma_start(out=V1.s(sl * 2048, 2048), in_=dap(vscr1_d, sl * 512 * 512, [[2048, 128], [1, 2048]])),
                  reads=[B_vscr1[sl]], writes=[B_V1[sl]])

    if STAGE >= 1:
        for gw in range(4):
            if "only3" in DBG and gw != 3:
                continue
            if "own4" in DBG or "own7" in DBG or "own5" in DBG:
                continue
            if "two" in DBG and gw < 2:
                continue
            if "first2" in DBG and gw >= 2:
                continue
            if "three" in DBG and gw < 1:
                continue
            if "bar" in DBG:
                T.barrier()
            window_prep(gw * 512, gw)
            if "preponly" in DBG:
                continue
            loads = [(wbf_d, 11, 8), (wbf_d, 14, 8)]
            kinds = [("k", 2), ("v", 2)]
            if gw == 3:
                loads += [(wbf_d, 9, 8), (wbf_d, 10, 8), (wbf_d, 12, 8), (wbf_d, 13, 8)]
                kinds += [("k", 0), ("k", 1), ("v", 0), ("v", 1)]

            def body(i, wb, Bw, kinds=kinds, gw=gw):
                kd, g = kinds[i]
                (k_chunk if kd == "k" else v_chunk)(g, gw, wb, Bw)
            run_loads(loads, body)

    if STAGE in (1, 2) and "only3" not in DBG and "haloonly" not in DBG:
        for gw in range(4, 8):
            if "own4" in DBG and gw != 4:
                continue
            if "own7" in DBG and gw != 7:
                continue
            if "own5" in DBG and gw != 5:
                continue
            window_prep(gw * 512, gw)
            kinds = [("k", 0), ("k", 1), ("k", 2), ("v", 0), ("v", 1), ("v", 2)]

            def body(i, wb, Bw, kinds=kinds, gw=gw):
                kd, g = kinds[i]
                (k_chunk if kd == "k" else v_chunk)(g, gw, wb, Bw)
            run_loads([(wbf_d, c, 8) for c in range(9, 15)], body)

    def retention_core(t):
        flush_pending()
        ps_sc, B_sc = [ps_num, ps_den], [B_num, B_den]

        def scores(h):
            qT_h = U1.s(O_qT + h * 512 + t * 128, 128)
            kT_h = U1.s(O_kTr + h * 512 + t * 128, 128)
            T.op("pe", lambda e: e.matmul(ps_sc[h % 2].s(0, 128), kT_h, qT_h, start=True, stop=True), reads=[B_seg[0], B_seg[1]], writes=[B_sc[h % 2]])
            jd = nxt("D")
            T.op("dve", lambda e: e.tensor_tensor(out=Dbf[jd].s(0, 128), in0=ps_sc[h % 2].s(0, 128), in1=cmask.s(0, 128), op=mult),
                 reads=[B_sc[h % 2], B_const], writes=[B_D[jd]])
            return jd
        jds = {0: scores(0)}
        for h in range(4):
            ps_r, B_r = ([ps_ret, ps_st][h % 2], [B_ret_o, B_st[0]][h % 2])
            qT_h = U1.s(O_qT + h * 512 + t * 128, 128)
            v_h = U1.s(O_rv + t * 1024 + h * 256, 256)
            if h + 1 < 4:
                jds[h + 1] = scores(h + 1)
            jd = jds[h]
            T.op("pe", lambda e: e.matmul(ps_r.s(128, 256), Dbf[jd].s(0, 128), v_h, start=True, stop=False, skip_group_check=True),
                 reads=[B_D[jd], B_seg[3]], writes=[B_r])
            T.op("pe", lambda e: e.matmul(ps_r.s(128, 256), qT_h, Sbf.s(h * 256, 256), start=False, stop=True, skip_group_check=True),
                 reads=[B_seg[0], B_Sbf[h]], writes=[B_r])
            state_update(t, h, True)
            gn_tail(U1.s(O_on + t * 1024 + h * 256, 256), ps_r, B_r)

    def gn_tail(dst_ap, ps_r=None, B_r=None):
        ps_r = ps_ret if ps_r is None else ps_r
        B_r = B_ret_o if B_r is None else B_r
        if True:
            o_ap = ps_r.s(128, 256)
            T.op("dve", lambda e: e.reduce_sum(out=small.s(32, 1), in_=o_ap, axis=AX.X), reads=[B_r], writes=[B_small])
            T.op("act", lambda e: e.activation(out=sq.s(0, 256), in_=o_ap, func=AF.Square), reads=[B_r], writes=[B_sq])
            T.op("dve", lambda e: e.reduce_sum(out=small.s(33, 1), in_=sq.s(0, 256), axis=AX.X), reads=[B_sq], writes=[B_small])
            T.op("dve", lambda e: e.tensor_scalar(out=small.s(34, 1), in0=small.s(32, 1), scalar1=1.0 / 256, scalar2=None, op0=mult),
                 reads=[B_small], writes=[B_small])
            T.op("dve", lambda e: e.tensor_tensor(out=small.s(35, 1), in0=small.s(34, 1), in1=small.s(34, 1), op=mult),
                 reads=[B_small], writes=[B_small])
            T.op("dve", lambda e: e.scalar_tensor_tensor(out=small.s(36, 1), in0=small.s(33, 1), scalar=1.0 / 256, in1=small.s(35, 1), op0=mult, op1=sub),
                 reads=[B_small], writes=[B_small])
            T.op("dve", lambda e: e.tensor_scalar(out=small.s(37, 1), in0=small.s(36, 1), scalar1=EPS, scalar2=None, op0=add),
                 reads=[B_small], writes=[B_small])
            T.op("act", lambda e: e.activation(out=small.s(37, 1), in_=small.s(37, 1), func=AF.Sqrt), reads=[B_small], writes=[B_small])
            T.op("dve", lambda e: e.reciprocal(out=small.s(37, 1), in_=small.s(37, 1)), reads=[B_small], writes=[B_small])
            T.op("dve", lambda e: e.scalar_tensor_tensor(out=small.s(38, 1), in0=small.s(34, 1), scalar=-1.0, in1=small.s(37, 1), op0=mult, op1=mult),
                 reads=[B_small], writes=[B_small])
            T.op("act", lambda e: e.activation(out=dst_ap, in_=o_ap, func=AF.Identity,
                                               scale=small.s(37, 1), bias=small.s(38, 1)), reads=[B_r, B_small], writes=[B_seg[4]])

    ps_sts = [ps_st, ps_ret]
    B_sts = [B_st[0], B_ret_sc]

    def attention(w):
        flush_pending()
        gw = 4 + w
        EBo = lambda g, h: (g * 8 + h) * 256
        jv2 = [0]

        def v2_load(pt_):
            jv = nxt("v2s")
            T.dma("sp", lambda e: e.dma_start(out=V2s[jv].ap(0, [[128, 16], [1, 128]]),
                                              in_=dap(vscr2_d, pt_ * 128, [[8192, 128], [512, 16], [1, 128]])),
                  reads=B_vscr2[0:4], writes=[B_V2s[jv]])
            nown = 32 * (w + 1)
            T.dma("sp", lambda e: e.dma_start(out=V2s[jv].ap(2048, [[128, 16], [1, 128]], p0=0, n=nown),
                                              in_=dap(vscr2_d, 2048 * 512 + pt_ * 128, [[8192, nown], [512, 16], [1, 128]])),
                  reads=B_vscr2[4:5 + w], writes=[B_V2s[jv]])
            return jv
        jv_next = [v2_load(0)]
        for h in range(8 if "heads4" not in DBG else 4):
            pt, po = h // 2, (h % 2) * 64
            first = [True, True]
            if h % 2 == 0:
                jv2[0] = jv_next[0]
                if pt + 1 < 4:
                    jv_next[0] = v2_load(pt + 1)

            def pv(rhs_ap, Brhs, v_ap, Bv, ones_ap, out_off, ncol, ostep):
                o_num = ps_num.ap(out_off, [[ostep, ncol]], p0=0, n=64)
                o_den = ps_den.ap(out_off, [[ostep, ncol]], p0=0, n=64)
                T.op("pe", lambda e: e.matmul(o_num, v_ap, rhs_ap, start=first[0], stop=False, skip_group_check=True),
                     reads=[Brhs] + Bv, writes=[B_num])
                first[0] = False
                T.op("pe", lambda e: e.matmul(o_den, ones_ap, rhs_ap, start=first[1], stop=False, skip_group_check=True),
                     reads=[Brhs, B_const], writes=[B_den])
                first[1] = False

            def softmax_piece(js, ncols, eb_ap):
                jp = nxt("pexp")
                T.op("act", lambda e: e.activation(out=pexp[jp].s(0, ncols), in_=ps_sts[js].s(0, ncols), func=AF.Exp),
                     reads=[B_sts[js]], writes=[B_pexp[jp]])
                T.op("dve", lambda e: e.tensor_tensor(out=pexp[jp].s(0, ncols), in0=pexp[jp].s(0, ncols), in1=eb_ap, op=mult),
                     reads=[B_pexp[jp], B_EB], writes=[B_pexp[jp]])
                return jp

            units = []

            def mk_g0(t):
                js, jp = nxt("st"), nxt("pexp")
                q_ap = U1.s(O_aqT + (0 * 4 + pt) * 512 + t * 128, 128, p0=po, n=64)
                cur = (gw * 4 + t) % 5
                prev = (gw * 4 + t - 1) % 5

                def qk():
                    for ki, slot in enumerate([prev, cur]):
                        k_ap = kT0.s(pt * 640 + slot * 128, 128, p0=po, n=64)
                        T.op("pe", lambda e: e.matmul(ps_sts[js].s(ki * 128, 128), k_ap, q_ap, start=True, stop=True, skip_group_check=True),
                             reads=[B_seg[0], B_seg[1], B_seg[2], B_kT0[slot]], writes=[B_sts[js]])

                def sm():
                    T.op("act", lambda e: e.activation(out=pexp[jp].s(0, 256), in_=ps_sts[js].s(0, 256), func=AF.Exp),
                         reads=[B_sts[js]], writes=[B_pexp[jp]])
                    T.op("dve", lambda e: e.tensor_tensor(out=pexp[jp].s(0, 256), in0=pexp[jp].s(0, 256), in1=EB.s(EBo(0, h), 256), op=mult),
                         reads=[B_pexp[jp], B_EB], writes=[B_pexp[jp]])

                def pvf():
                    for ki, slot in enumerate([prev, cur]):
                        halo = (gw == 4 and t == 0 and ki == 0)
                        pv(pexp[jp].s(ki * 128, 128), B_pexp[jp], V0.s(slot * 512 + h * 64, 64), [B_V0[slot]],
                           (hvb if halo else ones64).s(0, 64), t * 128, 128, 1)
                return qk, sm, pvf

            def mk_g1(r):
                js, jp = nxt("st"), nxt("pexp")
                q_ap = U1.ap(O_aqT + (1 * 4 + pt) * 512 + r, [[4, 128]], p0=po, n=64)
                sls = [(gw - 1) % 2, gw % 2]

                def qk():
                    for ki, sl in enumerate(sls):
                        k_ap = kT1.ap(pt * 1024 + sl * 512 + r, [[4, 128]], p0=po, n=64)
                        T.op("pe", lambda e: e.matmul(ps_sts[js].s(ki * 128, 128), k_ap, q_ap, start=True, stop=True, skip_group_check=True),
                             reads=[B_seg[0], B_seg[1], B_seg[2], B_kT1[sl]], writes=[B_sts[js]])

                def sm():
                    T.op("act", lambda e: e.activation(out=pexp[jp].s(0, 256), in_=ps_sts[js].s(0, 256), func=AF.Exp),
                         reads=[B_sts[js]], writes=[B_pexp[jp]])
                    T.op("dve", lambda e: e.tensor_tensor(out=pexp[jp].s(0, 256), in0=pexp[jp].s(0, 256), in1=EB.s(EBo(1, h), 256), op=mult),
                         reads=[B_pexp[jp], B_EB], writes=[B_pexp[jp]])

                def pvf():
                    for ki, sl in enumerate(sls):
                        halo = (gw == 4 and ki == 0)
                        pv(pexp[jp].s(ki * 128, 128), B_pexp[jp], V1.s(sl * 2048 + r * 512 + h * 64, 64), [B_V1[sl]],
                           (hvb if halo else ones64).s(0, 64), r, 128, 4)
                return qk, sm, pvf

            def mk_g2(rb):
                js, jp = nxt("st"), nxt("pexp")

                def qk():
                    for rr in range(4):
                        r = rb * 4 + rr
                        q_ap = U1.ap(O_aqT + (2 * 4 + pt) * 512 + r, [[16, 32]], p0=po, n=64)
                        for half in range(2):
                            k_ap = kT2.ap(pt * 4096 + half * 2048 + r, [[16, 128]], p0=po, n=64)
                            T.op("pe", lambda e: e.matmul(ps_sts[js].s(rr * 64 + half * 32, 32), k_ap, q_ap, start=True, stop=True, skip_group_check=True),
                                 reads=[B_seg[0], B_seg[1], B_seg[2]] + B_kT2, writes=[B_sts[js]])

                def sm():
                    eb_ap = EB.ap(EBo(2, h) + 32 * w, [[0, 4], [128, 2], [1, 32]])
                    T.op("act", lambda e: e.activation(out=pexp[jp].s(0, 256), in_=ps_sts[js].s(0, 256), func=AF.Exp),
                         reads=[B_sts[js]], writes=[B_pexp[jp]])
                    T.op("dve", lambda e: e.tensor_tensor(out=pexp[jp].ap(0, [[64, 4], [32, 2], [1, 32]]), in0=pexp[jp].ap(0, [[64, 4], [32, 2], [1, 32]]),
                                                          in1=eb_ap, op=mult), reads=[B_pexp[jp], B_EB], writes=[B_pexp[jp]])

                def pvf():
                    for rr in range(4):
                        r = rb * 4 + rr
                        for half in range(2):
                            pv(pexp[jp].s(rr * 64 + half * 32, 32), B_pexp[jp], V2s[jv2[0]].s(half * 2048 + r * 128 + (h % 2) * 64, 64), [B_V2s[jv2[0]]],
                               (hvb if half == 0 else ones64).s(0, 64), r, 32, 16)
                return qk, sm, pvf

            for t in range(4):
                units.append(mk_g0(t))
            for r in range(4):
                units.append(mk_g1(r))
            for rb in range(4):
                units.append(mk_g2(rb))
            units[0][0]()
            for k in range(len(units)):
                if k + 1 < len(units):
                    units[k + 1][0]()
                units[k][1]()
                units[k][2]()
            T.op("dve", lambda e: e.reciprocal(out=Rb.s(0, 512, n=64), in_=ps_den.s(0, 512, n=64)), reads=[B_den], writes=[B_Rb])
            T.op("dve", lambda e: e.tensor_tensor(out=tmpa.s(0, 512, n=64), in0=ps_num.s(0, 512, n=64),
                                                  in1=U1.s(O_sagT + pt * 512, 512, p0=po, n=64), op=mult),
                 reads=[B_num, B_seg[3]], writes=[B_tmpa])
            T.op("dve", lambda e: e.tensor_tensor(out=U1.s(O_gatT + pt * 512, 512, p0=po, n=64), in0=tmpa.s(0, 512, n=64),
                                                  in1=Rb.s(0, 512, n=64), op=mult), reads=[B_tmpa, B_Rb], writes=[B_seg[3]])

    if STAGE >= 3:
        for w in range(4):
            gw = 4 + w
            window_prep(OWN + w * 512, gw, cs_row0=3 * OWN + w * 512)
            def retA(i, wb, Bw):
                for t in range(4):
                    pm, Bpm = inproj_tile(wb, Bw, t)
                    if i == 0:
                        jb = nxt("kb", 6)
                        rotary(pm, Bpm, t, 0, kb[jb].ap(0, [[128, 4], [1, 128]]), [B_kb[jb]])
                        transposes_to(kb[jb], B_kb[jb], 0, 4, U1.ap(O_qT + t * 128, [[512, 4], [1, 128]]), [B_seg[0]])
                    elif i == 1:
                        rotary(pm, Bpm, t, 4, U1.ap(O_ktok + t * 512, [[128, 4], [1, 128]]), [B_seg[2]])
                        transposes_to(U1, B_seg[2], O_ktok + t * 512, 4, U1.ap(O_kTr + t * 128, [[512, 4], [1, 128]]), [B_seg[1]])
                    else:
                        v_epilogue(pm, Bpm, t, U1.s(O_rv + t * 1024 + (i - 2) * 512, 512), [B_seg[3]], False)
            run_loads([(wbf_d, c, 8) for c in range(4)], retA)
            for t in range(4):
                retention_core(t)

            def retB(i, wb, Bw):
                if i < 2:
                    for t in range(4):
                        pm, Bpm = inproj_tile(wb, Bw, t)
                        jf = nxt("kf")
                        T.op("act", lambda e: e.activation(out=kf[jf].s(0, 512), in_=pm.s(0, 512), func=AF.Silu, scale=rstd.s(t, 1)),
                             reads=[Bpm, B_rstd], writes=[B_kf[jf]])
                        o_ap = U1.s(O_on + t * 1024 + i * 512, 512)
                        T.op("dve", lambda e: e.tensor_tensor(out=o_ap, in0=o_ap, in1=kf[jf].s(0, 512), op=mult),
                             reads=[B_kf[jf], B_seg[4]], writes=[B_seg[4]])
                    if i == 1:
                        for t in range(4):
                            transposes_to(U1, B_seg[4], O_on + t * 1024, 8, U1.ap(O_gretT + t * 128, [[512, 8], [1, 128]]), [B_seg[0], B_seg[1]])
                elif i in (2, 4):
                    for t in range(4):
                        pm, Bpm = inproj_tile(wb, Bw, t)
                        T.op("act", lambda e: e.activation(out=sigb.s(t * 512, 512), in_=pm.s(0, 512), func=AF.Sigmoid, scale=rstd.s(t, 1)),
                             reads=[Bpm, B_rstd], writes=[B_sig])
                else:
                    j = (i - 3) // 2
                    for t in range(4):
                        pm, Bpm = inproj_tile(wb, Bw, t, 8, lhs=lambda kt: U1.s(O_gretT + kt * 512 + t * 128, 128), Blhs=[B_seg[0], B_seg[1]])
                        T.op("dve", lambda e: e.tensor_tensor(out=A.s(t * 1024 + j * 512, 512), in0=pm.s(0, 512), in1=sigb.s(t * 512, 512), op=mult),
                             reads=[Bpm, B_sig, B_seg[1]], writes=[B_A])
            run_loads([(wbf_d, 4, 8), (wbf_d, 5, 8), (wbf_d, 16, 8), (wprb_d, 0, 8), (wbf_d, 17, 8), (wprb_d, 1, 8)], retB)

            def attA(i, wb, Bw):
                c = 6 + i
                if c < 9:
                    g = c - 6
                    for t in range(4):
                        pm, Bpm = inproj_tile(wb, Bw, t)
                        jb, _ = head_rms(pm, Bpm, t, gq_tab, False)
                        transposes_to(kb[jb], B_kb[jb], 0, 4, U1.ap(O_aqT + g * 4 * 512 + t * 128, [[512, 4], [1, 128]]),
                                      [B_seg[0], B_seg[1], B_seg[2]])
                elif c < 12:
                    k_chunk(c - 9, gw, wb, Bw)
                elif c < 15:
                    v_chunk(c - 12, gw, wb, Bw)
                else:
                    for t in range(4):
                        pm, Bpm = inproj_tile(wb, Bw, t)
                        jb = nxt("kb", 6)
                        T.op("act", lambda e: e.activation(out=kb[jb].s(0, 512), in_=pm.s(0, 512), func=AF.Silu, scale=rstd.s(t, 1)),
                             reads=[Bpm, B_rstd], writes=[B_kb[jb]])
                        transposes_to(kb[jb], B_kb[jb], 0, 4, U1.ap(O_sagT + t * 128, [[512, 4], [1, 128]]), [B_seg[3]])
            run_loads([(wbf_d, c, 8) for c in range(6, 16)], attA)
            if "noatt" not in DBG:
                attention(w)

            def attB(i, wb, Bw):
                if i in (0, 2):
                    for t in range(4):
                        pm, Bpm = inproj_tile(wb, Bw, t)
                        T.op("act", lambda e: e.activation(out=sigb.s(t * 512, 512), in_=pm.s(0, 512), func=AF.Sigmoid, scale=rstd.s(t, 1)),
                             reads=[Bpm, B_rstd], writes=[B_sig])
                elif i in (1, 3):
                    j = (i - 1) // 2
                    for t in range(4):
                        pm, Bpm = inproj_tile(wb, Bw, t, 4, lhs=lambda kt: U1.s(O_gatT + kt * 512 + t * 128, 128), Blhs=[B_seg[3]])
                        jf = nxt("kf")
                        T.op("dve", lambda e: e.tensor_tensor(out=kf[jf].s(0, 512), in0=pm.s(0, 512), in1=sigb.s(t * 512, 512), op=mult),
                             reads=[Bpm, B_sig], writes=[B_kf[jf]])
                        T.op("dve", lambda e: e.tensor_tensor(out=U1.s(O_mrg + t * 1024 + j * 512, 512), in0=kf[jf].s(0, 512),
                                                               in1=A.s(t * 1024 + j * 512, 512), op=add),
                             reads=[B_kf[jf], B_A], writes=[B_seg[4]])
                    if i == 3:
                        for t in range(4):
                            transposes_to(U1, B_seg[4], O_mrg + t * 1024, 8, U1.ap(O_mT + t * 128, [[512, 8], [1, 128]]), [B_seg[0], B_seg[1]])
                else:
                    j = i - 4
                    for t in range(4):
                        pm, Bpm = inproj_tile(wb, Bw, t, 8, lhs=lambda kt: U1.s(O_mT + kt * 512 + t * 128, 128), Blhs=[B_seg[0], B_seg[1]])
                        jx = nxt("xin")
                        T.dma("sp", lambda e: e.dma_start(out=xin[jx].s(0, 512), in_=dap(x_parts[1], (w * 512 + t * 128) * D + j * 512, [[D, 128], [1, 512]])),
                              writes=[B_xin[jx]])
                        jo = nxt("of")
                        T.op("dve", lambda e: e.tensor_tensor(out=of32[jo].s(0, 512), in0=pm.s(0, 512), in1=xin[jx].s(0, 512), op=add),
                             reads=[Bpm, B_xin[jx], B_seg[1]], writes=[B_of[jo]])
                        T.dma(QST, lambda e: e.dma_start(out=dap(y_d, (w * 512 + t * 128) * D + j * 512, [[D, 128], [1, 512]]), in_=of32[jo].s(0, 512)),
                              reads=[B_of[jo]], writes=[Buf()])
            run_loads([(wbf_d, 18, 8), (wpab_d, 0, 4), (wbf_d, 19, 8), (wpab_d, 1, 4), (wob_d, 0, 8), (wob_d, 1, 8)], attB)

    if STAGE >= 2 and "nosample" not in DBG:
        decs = sb("decs", 16)
        kms = [sb("kms0", 128, BF16), sb("kms1", 128, BF16)]
        B_kms = [Buf("kms0"), Buf("kms1")]
        smask_b = sb("smask_b", 128, BF16)
        colmask_b = sb("colmask_b", 512, BF16)
        EBs = sb("EBs", 432)
        onesf = sb("onesf", 1)
        s0b = [sb("s0b0", 256, BF16), sb("s0b1", 256, BF16)]
        B_s0b = [Buf("s0b0"), Buf("s0b1")]
        qm = sb("qm", 512, BF16)
        B_qm = Buf("qm")
        segs012 = [B_seg[0], B_seg[1], B_seg[2]]
        T.dma("sp", lambda e: e.dma_start(out=decs.s(0, 16), in_=dap(decs_d, 0, [[16, 128], [1, 16]])), writes=[B_const])
        j = nxt("kf")
        T.dma("sp", lambda e: e.dma_start(out=kf[j].s(0, 128), in_=dap(smask_d, 0, [[128, 128], [1, 128]])), writes=[B_kf[j]])
        T.op("dve", lambda e: e.tensor_copy(out=smask_b.s(0, 128), in_=kf[j].s(0, 128)), reads=[B_kf[j]], writes=[B_const])
        j = nxt("kf")
        T.dma("sp", lambda e: e.dma_start(out=kf[j].s(0, 512), in_=dap(colmask_d, 0, [[512, 128], [1, 512]])), writes=[B_kf[j]])
        T.op("dve", lambda e: e.tensor_copy(out=colmask_b.s(0, 512), in_=kf[j].s(0, 512)), reads=[B_kf[j]], writes=[B_const])
        j0 = nxt("kf")
        T.dma("sp", lambda e: e.dma_start(out=kf[j0].s(0, 432), in_=dap(bts_d, 0, [[432, 128], [1, 432]])), writes=[B_kf[j0]])
        T.op("act", lambda e: e.activation(out=kf[j0].s(0, 432), in_=kf[j0].s(0, 432), func=AF.Exp), reads=[B_kf[j0]], writes=[B_kf[j0]])
        j1 = nxt("kf")
        T.dma("sp", lambda e: e.dma_start(out=kf[j1].s(0, 432), in_=dap(vms_d, 0, [[432, 128], [1, 432]])), writes=[B_kf[j1]])
        T.op("dve", lambda e: e.tensor_tensor(out=EBs.s(0, 432), in0=kf[j0].s(0, 432), in1=kf[j1].s(0, 432), op=mult),
             reads=[B_kf[j0], B_kf[j1]], writes=[B_const])
        T.op("dve", lambda e: e.memset(onesf.s(0, 1), 1.0), writes=[B_const])
        window_prep(0, 0, src=xs_d, ntiles=1)
        T.dma("sp", lambda e: e.dma_start(out=cosw.s(0, 64), in_=dap(coss_d, 0, [[64, 128], [1, 64]])), writes=[B_cs])
        T.dma("sp", lambda e: e.dma_start(out=sinw.s(0, 64), in_=dap(sins_d, 0, [[64, 128], [1, 64]])), writes=[B_cs])

        def sA(i, wb, Bw):
            pm, Bpm = inproj_tile(wb, Bw, 0)
            if i == 0:
                jb = nxt("kb", 6)
                rotary(pm, Bpm, 0, 12, kb[jb].ap(0, [[128, 4], [1, 128]]), [B_kb[jb]], dec_sb=decs)
                transposes_to(kb[jb], B_kb[jb], 0, 4, U1.ap(O_qT, [[512, 4], [1, 128]]), [B_seg[0]])
            elif i == 1:
                rotary(pm, Bpm, 0, 0, U1.ap(O_ktok, [[128, 4], [1, 128]]), [B_seg[2]], dec_sb=decs)
                transposes_to(U1, B_seg[2], O_ktok, 4, U1.ap(O_kTr, [[512, 4], [1, 128]]), [B_seg[1]])
            else:
                v_epilogue(pm, Bpm, 0, U1.s(O_rv + (i - 2) * 512, 512), [B_seg[3]], False)
        run_loads([(wbf_d, c, 8) for c in range(4)], sA)

        for h in range(4):
            qT_h = U1.s(O_qT + h * 512, 128)
            kT_h = U1.s(O_kTr + h * 512, 128)
            v_h = U1.s(O_rv + h * 256, 256)
            T.op("pe", lambda e: e.matmul(ps_ret.s(0, 128), kT_h, qT_h, start=True, stop=True), reads=[B_seg[0], B_seg[1]], writes=[B_ret_sc])
            jd = nxt("D")
            T.op("dve", lambda e: e.tensor_tensor(out=Dbf[jd].s(0, 128), in0=ps_ret.s(0, 128), in1=smask_b.s(0, 128), op=mult),
                 reads=[B_ret_sc, B_const], writes=[B_D[jd]])
            T.op("dve", lambda e: e.tensor_tensor(out=qm.ap(0, [[128, 4], [1, 128]]), in0=U1.ap(O_qT + h * 512, [[0, 4], [1, 128]]),
                                                  in1=colmask_b.ap(0, [[128, 4], [1, 128]]), op=mult),
                 reads=[B_seg[0], B_const], writes=[B_qm])
            T.op("pe", lambda e: e.matmul(ps_ret.s(128, 256), Dbf[jd].s(0, 128), v_h, start=True, stop=False, skip_group_check=True),
                 reads=[B_D[jd], B_seg[3]], writes=[B_ret_o])
            for bl in range(4):
                jx = nxt("xin")
                T.dma("sp", lambda e: e.dma_start(out=xin[jx].s(0, 256), in_=dap(s0_d, (bl * 4 + h) * 128 * 256, [[256, 128], [1, 256]])),
                      writes=[B_xin[jx]])
                js = bl % 2
                T.op("act", lambda e: e.activation(out=s0b[js].s(0, 256), in_=xin[jx].s(0, 256), func=AF.Copy), reads=[B_xin[jx]], writes=[B_s0b[js]])
                T.op("pe", lambda e: e.matmul(ps_ret.s(128, 256), qm.s(bl * 128, 128), s0b[js].s(0, 256), start=False, stop=(bl == 3), skip_group_check=True),
                     reads=[B_qm, B_s0b[js]], writes=[B_ret_o])
                jk = (bl * 4 + h) % 2
                T.op("dve", lambda e: e.tensor_scalar(out=kms[jk].s(0, 128), in0=U1.s(O_ktok + h * 128, 128), scalar1=decs.s(8 + bl, 1),
                                                       scalar2=None, op0=mult), reads=[B_seg[2], B_const], writes=[B_kms[jk]])
                T.op("pe", lambda e: e.matmul(ps_misc.s(0, 256), kms[jk].s(0, 128), v_h, start=True, stop=True),
                     reads=[B_kms[jk], B_seg[3]], writes=[B_misc])
                jo = nxt("of")
                T.op("dve", lambda e: e.tensor_tensor(out=of32[jo].s(0, 256), in0=ps_misc.s(0, 256), in1=xin[jx].s(0, 256), op=add),
                     reads=[B_misc, B_xin[jx]], writes=[B_of[jo]])
                T.op("dve", lambda e: e.tensor_scalar(out=of32[jo].s(0, 256), in0=of32[jo].s(0, 256), scalar1=decs.s(4 + h, 1), scalar2=None, op0=mult),
                     reads=[B_of[jo], B_const], writes=[B_of[jo]])
                T.dma("sp", lambda e: e.dma_start(out=dap(rss_d, (bl * 4 + h) * 128 * 256, [[256, 128], [1, 256]]), in_=of32[jo].s(0, 256)),
                      reads=[B_of[jo]], writes=[Buf()])
            gn_tail(U1.s(O_on + h * 256, 256))

        def sB(i, wb, Bw):
            if i < 2:
                pm, Bpm = inproj_tile(wb, Bw, 0)
                jf = nxt("kf")
                T.op("act", lambda e: e.activation(out=kf[jf].s(0, 512), in_=pm.s(0, 512), func=AF.Silu, scale=rstd.s(0, 1)),
                     reads=[Bpm, B_rstd], writes=[B_kf[jf]])
                o_ap = U1.s(O_on + i * 512, 512)
                T.op("dve", lambda e: e.tensor_tensor(out=o_ap, in0=o_ap, in1=kf[jf].s(0, 512), op=mult), reads=[B_kf[jf], B_seg[4]], writes=[B_seg[4]])
                if i == 1:
                    transposes_to(U1, B_seg[4], O_on, 8, U1.ap(O_gretT, [[512, 8], [1, 128]]), [B_seg[0], B_seg[1]])
            elif i in (2, 4):
                pm, Bpm = inproj_tile(wb, Bw, 0)
                T.op("act", lambda e: e.activation(out=sigb.s(0, 512), in_=pm.s(0, 512), func=AF.Sigmoid, scale=rstd.s(0, 1)),
                     reads=[Bpm, B_rstd], writes=[B_sig])
            else:
                j = (i - 3) // 2
                pm, Bpm = inproj_tile(wb, Bw, 0, 8, lhs=lambda kt: U1.s(O_gretT + kt * 512, 128), Blhs=[B_seg[0], B_seg[1]])
                T.op("dve", lambda e: e.tensor_tensor(out=A.s(j * 512, 512), in0=pm.s(0, 512), in1=sigb.s(0, 512), op=mult),
                     reads=[Bpm, B_sig], writes=[B_A])
        run_loads([(wbf_d, 4, 8), (wbf_d, 5, 8), (wbf_d, 16, 8), (wprb_d, 0, 8), (wbf_d, 17, 8), (wprb_d, 1, 8)], sB)

        O_qn, O_kn, O_vn, O_sg = O_aqT, O_aqT + 1536, O_aqT + 3072, O_aqT + 4608

        def sC(i, wb, Bw):
            c = 6 + i
            pm, Bpm = inproj_tile(wb, Bw, 0)
            if c < 9:
                jb, _ = head_rms(pm, Bpm, 0, gq_tab, False)
                T.op("dve", lambda e: e.tensor_copy(out=U1.s(O_qn + (c - 6) * 512, 512), in_=kb[jb].s(0, 512)), reads=[B_kb[jb]], writes=segs012)
            elif c < 12:
                jb, jo = head_rms(pm, Bpm, 0, gk_tab, True)
                kv_store(jo, kvs_d[c - 9], 0, 0)
                T.op("dve", lambda e: e.tensor_copy(out=U1.s(O_kn + (c - 9) * 512, 512), in_=kb[jb].s(0, 512)), reads=[B_kb[jb]], writes=segs012)
            elif c < 15:
                jo = v_epilogue(pm, Bpm, 0, U1.s(O_vn + (c - 12) * 512, 512), segs012, True)
                kv_store(jo, kvs_d[c - 12], 0, 1)
            else:
                T.op("act", lambda e: e.activation(out=U1.s(O_sg, 512), in_=pm.s(0, 512), func=AF.Silu, scale=rstd.s(0, 1)),
                     reads=[Bpm, B_rstd], writes=segs012)
        run_loads([(wbf_d, c, 8) for c in range(6, 16)], sC)

        B_att = Buf("att_scr")
        sm2 = sb("sm2", 64)
        B_sm2 = [Buf("sm2a"), Buf("sm2b")]
        B_sq2 = B_sqh
        B_rot2x = [Buf("rot2a"), Buf("rot2b")]
        T.op("dve", lambda e: e.memset(sm2.s(0, 64), 0.0), reads=[B_rot], writes=B_sm2 + B_rot2x)
        un = [0]
        nrows = [128, 512, 2048]
        dil = [1, 4, 16]
        for p in range(16):
            bl, i = p // 4, p % 4
            first = True
            for g in range(3):
                m = nxt("mm")
                T.op("pe", lambda e: e.matmul(ps_mm[m].s(0, 512), ident_b.ap(p, [[0, 128]]), U1.s(O_qn + g * 512, 512), start=True, stop=True),
                     reads=[B_const] + segs012, writes=[B_mm[m]])
                for tile in range(2):
                    if tile == 0:
                        jx = nxt("xin")
                        base = (bl * nrows[g] + (i if g > 0 else 0)) * 1024
                        T.dma("sp", lambda e: e.dma_start(out=xin[jx].s(0, 512), in_=dap(cch_d[g], base, [[dil[g] * 1024, 128], [1, 512]])),
                              writes=[B_xin[jx]])
                        T.dma("sp", lambda e: e.dma_start(out=xin[jx].s(512, 512), in_=dap(cch_d[g], base + 512, [[dil[g] * 1024, 128], [1, 512]])),
                              writes=[B_xin[jx]])
                        K_ap, V_ap, BK = xin[jx].ap(0, [[64, 8], [1, 64]]), xin[jx].ap(512, [[64, 8], [1, 64]]), [B_xin[jx]]
                        tcol = (i if g == 0 else 3 + g) * 8
                    else:
                        K_ap, V_ap, BK = U1.ap(O_kn + g * 512, [[64, 8], [1, 64]]), U1.ap(O_vn + g * 512, [[64, 8], [1, 64]]), segs012
                        tcol = (6 + g * 16 + p) * 8
                    u2 = un[0] % 2
                    un[0] += 1
                    so, lo = u2 * 512, u2 * 16
                    T.op("dve", lambda e: e.tensor_tensor(out=sq.ap(so, [[64, 8], [1, 64]]), in0=K_ap, in1=ps_mm[m].ap(0, [[64, 8], [1, 64]]), op=mult),
                         reads=BK + [B_mm[m]], writes=[B_sq2[u2]])
                    T.op("dve", lambda e: e.reduce_sum(out=sm2.s(lo, 8), in_=sq.ap(so, [[64, 8], [1, 64]]), axis=AX.X), reads=[B_sq2[u2]], writes=[B_sm2[u2]])
                    T.op("act", lambda e: e.activation(out=sm2.s(lo + 8, 8), in_=sm2.s(lo, 8), func=AF.Exp), reads=[B_sm2[u2]], writes=[B_sm2[u2]])
                    T.op("dve", lambda e: e.tensor_tensor(out=sm2.s(lo + 8, 8), in0=sm2.s(lo + 8, 8), in1=EBs.s(tcol, 8), op=mult),
                         reads=[B_sm2[u2], B_const], writes=[B_sm2[u2]])
                    T.op("dve", lambda e: e.tensor_tensor(out=rot.ap(so, [[64, 8], [1, 64]]), in0=V_ap, in1=sm2.ap(lo + 8, [[1, 8], [0, 64]]), op=mult),
                         reads=BK + [B_sm2[u2]], writes=[B_rot2x[u2]])
                    T.op("pe", lambda e: e.matmul(ps_num.s(0, 512, n=1), onesf.s(0, 1), rot.s(so, 512), start=first, stop=False, skip_group_check=True),
                         reads=[B_rot2x[u2], B_const], writes=[B_num])
                    T.op("pe", lambda e: e.matmul(ps_den.s(0, 8, n=1), onesf.s(0, 1), sm2.s(lo + 8, 8), start=first, stop=False, skip_group_check=True),
                         reads=[B_sm2[u2], B_const], writes=[B_den])
                    first = False
            T.op("dve", lambda e: e.reciprocal(out=small.s(56, 8, n=1), in_=ps_den.s(0, 8, n=1)), reads=[B_den], writes=[B_small])
            jf = nxt("kf")
            T.op("dve", lambda e: e.tensor_tensor(out=kf[jf].ap(0, [[64, 8], [1, 64]], n=1), in0=ps_num.ap(0, [[64, 8], [1, 64]], n=1),
                                                  in1=small.ap(56, [[1, 8], [0, 64]], n=1), op=mult), reads=[B_num, B_small], writes=[B_kf[jf]])
            T.dma("sp", lambda e: e.dma_start(out=dap(att_scr_d, p * 512, [[512, 1], [1, 512]]), in_=kf[jf].s(0, 512, n=1)),
                  reads=[B_kf[jf]], writes=[B_att])
        jf = nxt("kf")
        T.dma("sp", lambda e: e.dma_start(out=kf[jf].s(0, 512, n=16), in_=dap(att_scr_d, 0, [[512, 16], [1, 512]])), reads=[B_att], writes=[B_kf[jf]])
        jb = nxt("kb", 6)
        T.op("dve", lambda e: e.memset(kb[jb].s(0, 512), 0.0), writes=[B_kb[jb]])
        T.op("dve", lambda e: e.tensor_tensor(out=kb[jb].s(0, 512, n=16), in0=kf[jf].s(0, 512, n=16), in1=U1.s(O_sg, 512, n=16), op=mult),
             reads=[B_kf[jf]] + segs012, writes=[B_kb[jb]])
        transposes_to(kb[jb], B_kb[jb], 0, 4, U1.ap(O_gatT, [[512, 4], [1, 128]]), [B_seg[3]])

        def sD(i, wb, Bw):
            if i in (0, 2):
                pm, Bpm = inproj_tile(wb, Bw, 0)
                T.op("act", lambda e: e.activation(out=sigb.s(0, 512), in_=pm.s(0, 512), func=AF.Sigmoid, scale=rstd.s(0, 1)),
                     reads=[Bpm, B_rstd], writes=[B_sig])
            elif i in (1, 3):
                j = (i - 1) // 2
                pm, Bpm = inproj_tile(wb, Bw, 0, 4, lhs=lambda kt: U1.s(O_gatT + kt * 512, 128), Blhs=[B_seg[3]])
                jf = nxt("kf")
                T.op("dve", lambda e: e.tensor_tensor(out=kf[jf].s(0, 512), in0=pm.s(0, 512), in1=sigb.s(0, 512), op=mult),
                     reads=[Bpm, B_sig], writes=[B_kf[jf]])
                T.op("dve", lambda e: e.tensor_tensor(out=U1.s(O_mrg + j * 512, 512), in0=kf[jf].s(0, 512), in1=A.s(j * 512, 512), op=add),
                     reads=[B_kf[jf], B_A], writes=[B_seg[4]])
                if i == 3:
                    transposes_to(U1, B_seg[4], O_mrg, 8, U1.ap(O_mT, [[512, 8], [1, 128]]), [B_seg[0], B_seg[1]])
            else:
                j = i - 4
                pm, Bpm = inproj_tile(wb, Bw, 0, 8, lhs=lambda kt: U1.s(O_mT + kt * 512, 128), Blhs=[B_seg[0], B_seg[1]])
                jx = nxt("xin")
                T.dma("sp", lambda e: e.dma_start(out=xin[jx].s(0, 512), in_=dap(xs_d, j * 512, [[D, 128], [1, 512]])), writes=[B_xin[jx]])
                jo = nxt("of")
                T.op("dve", lambda e: e.tensor_tensor(out=of32[jo].s(0, 512), in0=pm.s(0, 512), in1=xin[jx].s(0, 512), op=add),
                     reads=[Bpm, B_xin[jx]], writes=[B_of[jo]])
                T.dma("sp", lambda e: e.dma_start(out=dap(ys_d, j * 512, [[D, 16], [1, 512]]), in_=of32[jo].s(0, 512, n=16)),
                      reads=[B_of[jo]], writes=[Buf()])
        run_loads([(wbf_d, 18, 8), (wpab_d, 0, 4), (wbf_d, 19, 8), (wpab_d, 1, 4), (wob_d, 0, 8), (wob_d, 1, 8)], sD)

    if STAGE >= 2:
        T.dma(QST, lambda e: e.dma_start(out=dap(sfin_d, 0, [[256, 128], [128 * 256, 4], [1, 256]]), in_=S.ap(0, [[256, 4], [1, 256]])),
              reads=B_S, writes=[Buf()])

    if "pad" in DBG:
        for i in range(int(os.environ.get("MK_PAD", "2000"))):
            if "padnoinc" in DBG and i > 0:
                nc.tensor.matmul(ps_st.s(0, 32), ident_b.s(0, 128), ident_b.s(0, 32), start=True, stop=True, skip_group_check=True)
                continue
            T.op("pe", lambda e: e.matmul(ps_st.s(0, 32), ident_b.s(0, 128), ident_b.s(0, 32), start=True, stop=True, skip_group_check=True),
                 reads=[B_const], writes=[B_st[0]])
    T.finish()
    es.close()
    print("[kernel] sbuf bytes/partition (persistent): %d ; waits=%d ; counts=%s ; waits_by_eng=%s ; dmas=%d" % (
        sbuf_used[0], T.n_wait, {k: v.count for k, v in T.e.items()}, {k: getattr(v, "nw", 0) for k, v in T.e.items()}, T.di))
    return nc


def t5_bucket_np(dist):
    max_exact = 16
    d = np.maximum(dist.astype(np.float32), np.float32(1.0))
    large = max_exact + (np.log(d / np.float32(max_exact)) / np.float32(np.log(2048 / 16)) * np.float32(16)).astype(np.int32)
    large = np.minimum(large, 31)
    return np.where(dist < max_exact, dist, large)


def host_constants():
    c = {}
    c["ident"] = np.eye(128, dtype=np.float32)
    kq = np.arange(128)
    c["cmask"] = (kq[None, :] >= kq[:, None]).astype(np.float32)
    j = np.arange(128)[:, None]
    cc = np.arange(256)[None, :]
    m = np.where(cc < 128, 128 + cc - j, cc - 128 - j)
    valid = (m >= 0) & (m <= 128)
    c["tmask"] = valid.astype(np.float32)
    c["m_idx"] = np.clip(m, 0, 128)
    H = 4
    log_g = np.log1p(-(2.0 ** (-5.0 - np.arange(H, dtype=np.float64))))
    i = np.arange(128, dtype=np.float64)[:, None]
    qdec = np.exp((i + 1.0) * log_g[None, :])
    kdec = np.exp(-(i + 1.0) * log_g[None, :]) * (128 ** -0.5)
    cd = np.exp(128.0 * log_g)[None, :].repeat(128, 0)
    c["dec"] = np.concatenate([qdec, kdec, cd, kdec * cd], axis=1).astype(np.float32)
    c["log_g"] = log_g
    return c


def kernel(x_prompt, x_sample, cache_kv_w128, cache_kv_w512, cache_kv_w2048, state_retention,
           w_norm, w_in, q_norm, k_norm, rel_bias, ret_norm, w_proj_ret, w_proj_att, w_out):
    f32 = np.float32
    x_prompt = np.asarray(x_prompt, f32)
    hc = host_constants()
    nc = build_program()

    w_in0 = np.ascontiguousarray(np.asarray(w_in, f32)[0])
    gT = np.ascontiguousarray(np.asarray(w_norm, f32)[0].reshape(8, 128).T)
    rnT = np.ascontiguousarray(np.asarray(ret_norm, f32)[0].reshape(8, 128).T)
    gq_tab = np.ascontiguousarray(np.broadcast_to(np.asarray(q_norm, f32)[0][None, :], (128, 64)))
    gk_tab = np.ascontiguousarray(np.broadcast_to(np.asarray(k_norm, f32)[0][None, :], (128, 64)))
    rb = np.asarray(rel_bias, f32)
    dils = [1, 4, 16]
    bt = np.zeros((128, 24, 256), f32)
    for g in range(3):
        bidx = t5_bucket_np((dils[g] * hc["m_idx"]).astype(np.int32))
        for h in range(8):
            bt[:, g * 8 + h, :] = rb[bidx, g * 8 + h]
    bt = bt.reshape(128, 24 * 256)
    half = 32
    inv = (np.float32(10000.0) ** (-np.arange(half * 2, dtype=f32) / np.float32(half * 2)))
    xs_all = np.asarray(x_sample, f32).reshape(128, D)
    sr = np.asarray(state_retention, f32)[0]
    lg = hc["log_g"]
    i_of = (np.arange(128) % 4).astype(np.float64)[:, None]
    kdec_s = np.exp(-(i_of + 1.0) * lg[None, :]) * (128 ** -0.5)
    g4 = np.exp(4.0 * lg)[None, :].repeat(128, 0)
    bmask = np.zeros((128, 4))
    for bl in range(4):
        bmask[4 * bl:4 * bl + 4, bl] = 1.0
    qdec_s = np.exp((i_of + 1.0) * lg[None, :])
    dec_s = np.concatenate([kdec_s, g4, bmask, qdec_s], axis=1).astype(f32)
    tk = np.arange(128)
    smask = ((tk[:, None] // 4 == tk[None, :] // 4) & (tk[None, :] % 4 >= tk[:, None] % 4)).astype(f32)
    colmask = np.zeros((128, 4, 128), f32)
    for bl in range(4):
        colmask[:, bl, 4 * bl:4 * bl + 4] = 1.0
    colmask = colmask.reshape(128, 512)
    bts = np.zeros((128, 54, 8), f32)
    vms = np.zeros((128, 54, 8), f32)
    kk = np.arange(128)
    for i in range(4):
        mm = 128 + i - kk
        ok = mm <= 128
        bts[:, i, :] = rb[t5_bucket_np(np.clip(mm, 0, 128).astype(np.int32)), 0:8]
        vms[:, i, :] = ok[:, None]
    for g in (1, 2):
        mm = 128 - kk
        bts[:, 3 + g, :] = rb[t5_bucket_np((dils[g] * mm).astype(np.int32)), g * 8:(g + 1) * 8]
        vms[:, 3 + g, :] = 1.0
    for g in range(3):
        for p in range(16):
            bl, i = p // 4, p % 4
            for k in range(16):
                bk, j = k // 4, k % 4
                if bk != bl:
                    continue
                if (g == 0 and j <= i) or (g > 0 and j == i):
                    m_ = i - j
                    bts[k, 6 + g * 16 + p, :] = rb[int(t5_bucket_np(np.array([dils[g] * m_], np.int32))[0]), g * 8:(g + 1) * 8]
                    vms[k, 6 + g * 16 + p, :] = 1.0
    bts = bts.reshape(128, 432)
    vms = vms.reshape(128, 432)
    caches = [np.asarray(cache_kv_w128, f32)[0], np.asarray(cache_kv_w512, f32)[0], np.asarray(cache_kv_w2048, f32)[0]]
    pos_s = (16384 + (np.arange(128) % 4)).astype(f32)
    ang_s = pos_s[:, None] * inv[None, :]
    in_maps = []
    for c in range(NCORES):
        b, p = c // 4, c % 4
        own = x_prompt[b, p * OWN:(p + 1) * OWN]
        halo = x_prompt[b, (p - 1) * OWN:p * OWN] if p > 0 else np.zeros((OWN, D), f32)
        pos = np.maximum((p - 3) * OWN + np.arange(4 * OWN), 0).astype(f32)
        ang = pos[:, None] * inv[None, :]
        xpp = np.zeros((3 * OWN, D), f32)
        if p > 0:
            xpp[(3 - p) * OWN:] = x_prompt[b, :p * OWN]
        in_maps.append({
            "x_h": np.ascontiguousarray(halo), "x_o": np.ascontiguousarray(own),
            "w_in": w_in0,
            "w_proj_ret": np.ascontiguousarray(np.asarray(w_proj_ret, f32)[0]),
            "w_proj_att": np.ascontiguousarray(np.asarray(w_proj_att, f32)[0]),
            "w_out": np.ascontiguousarray(np.asarray(w_out, f32)[0]),
            "gT": gT, "rnT": rnT, "gq_tab": gq_tab, "gk_tab": gk_tab,
            "bt": bt, "tmask": hc["tmask"],
            "cos": np.cos(ang).astype(f32), "sin": np.sin(ang).astype(f32),
            "dec": hc["dec"], "cmask": hc["cmask"], "ident": hc["ident"],
            "hv": np.full((128, 64), 1.0 if p > 0 else 0.0, f32),
            "x_pp": xpp,
            "x_s": np.ascontiguousarray(np.roll(xs_all, -16 * c, axis=0)),
            "cos_s": np.cos(ang_s).astype(f32), "sin_s": np.sin(ang_s).astype(f32), "dec_s": dec_s,
            "s0": np.ascontiguousarray(sr[4 * c:4 * c + 4].reshape(16, 128, 256)),
            "smask": smask, "colmask": colmask, "bts": bts, "vms": vms,
            "c128": np.ascontiguousarray(caches[0][4 * c:4 * c + 4].reshape(4 * 128, 2, 512)),
            "c512": np.ascontiguousarray(caches[1][4 * c:4 * c + 4].reshape(4 * 512, 2, 512)),
            "c2048": np.ascontiguousarray(caches[2][4 * c:4 * c + 4].reshape(4 * 2048, 2, 512)),
        })
    res = run_bass_kernel_spmd(nc, in_maps, core_ids=list(range(NCORES)))
    R = res.results
    y_p = np.stack([np.concatenate([R[b * 4 + p]["y"] for p in range(4)], 0) for b in range(2)], 0)
    rs_p = np.stack([R[b * 4 + 3]["sfin"].reshape(4, 128, 256) for b in range(2)], 0)[None]
    kv128 = np.stack([R[b * 4 + 3]["kv128"].reshape(128, 2, 8, 64) for b in range(2)], 0)[None]
    kv512 = np.stack([R[b * 4 + 3]["kv512"].reshape(512, 2, 8, 64) for b in range(2)], 0)[None]
    kv2048 = np.stack([R[b * 4 + 3]["kv2048"].reshape(2048, 2, 8, 64) for b in range(2)], 0)[None]
    y_s = np.stack([R[c]["ys"].reshape(4, 4, 1024) for c in range(NCORES)], 0).reshape(32, 4, 1024)
    rs_s = np.stack([R[c]["rss"].reshape(4, 4, 128, 256) for c in range(NCORES)], 0).reshape(1, 32, 4, 128, 256)
    kvs = [R[0]["kvs%d" % g].reshape(1, 32, 4, 2, 8, 64) for g in range(3)]
    return (y_p, y_s, rs_p, rs_s, kv128, kv512, kv2048, kvs[0], kvs[1], kvs[2])
```

```python
import os
import numpy as np
from contextlib import ExitStack
import concourse.bass as bass
import concourse.mybir as mybir
from concourse.bass_utils import run_bass_kernel_spmd

F32 = mybir.dt.float32
BF16 = mybir.dt.bfloat16
ALU = mybir.AluOpType
AF = mybir.ActivationFunctionType
AX = mybir.AxisListType

NCORES = 8
D = 1024
SEQ = 8192
OWN = 2048
INW = 10240
EPS = 1e-6
NDMA = 40
STAGE = int(os.environ.get("MK_STAGE", "9"))
DBG = os.environ.get("MK_DBG", "")
QST = os.environ.get("MK_QST", "sp")


class Ev:
    __slots__ = ("key", "seq", "sem", "val", "front")

    def __init__(self, key, seq, sem, val, front=None):
        self.key, self.seq, self.sem, self.val = key, seq, sem, val
        self.front = front or {}


class Buf:
    __slots__ = ("w", "r", "name", "excl")

    def __init__(self, name="", excl=False):
        self.w = None
        self.r = {}
        self.name = name
        self.excl = excl


class Eng:
    def __init__(self, name, h, sem, inorder=False):
        self.name, self.h, self.sem, self.inorder = name, h, sem, inorder
        self.count = 0
        self.waited = {}


class Tracker:
    def __init__(self, nc, es):
        self.nc = nc
        self.e = {}
        for name, h, io in [("pe", nc.tensor, True), ("act", nc.scalar, False), ("dve", nc.vector, False),
                            ("pool", nc.gpsimd, False), ("sp", nc.sync, False)]:
            self.e[name] = Eng(name, h, es.enter_context(nc.semaphore("s_" + name)), io)
        self.dsem = [es.enter_context(nc.semaphore("d%d" % i)) for i in range(NDMA)]
        self.duse = [0] * NDMA
        self.di = 0
        self.dma_evs = []
        self.n_wait = 0

    def _wait(self, e, ev):
        if ev is None:
            return
        if ev.key == e.name and e.inorder:
            return
        if e.waited.get(ev.key, 0) >= ev.seq:
            return
        e.h.wait_ge(ev.sem, ev.val)
        self.n_wait += 1
        e.nw = getattr(e, "nw", 0) + 1
        e.waited[ev.key] = ev.seq
        for k, v in ev.front.items():
            if e.waited.get(k, 0) < v:
                e.waited[k] = v

    def _deps(self, e, reads, writes):
        for b in reads:
            self._wait(e, b.w)
            if b.excl:
                for ev in list(b.r.values()):
                    if ev.key != e.name:
                        self._wait(e, ev)
        for b in writes:
            self._wait(e, b.w)
            for ev in list(b.r.values()):
                self._wait(e, ev)

    def op(self, eng, fn, reads=(), writes=()):
        e = self.e[eng]
        self._deps(e, reads, writes)
        inst = fn(e.h)
        e.count += 1
        inst.then_inc(e.sem, 1)
        ev = Ev(e.name, e.count, e.sem, e.count, dict(e.waited))
        for b in reads:
            b.r[e.name] = ev
        for b in writes:
            b.w = ev
            b.r = {}
        return ev

    def dma(self, q, fn, reads=(), writes=()):
        e = self.e[q]
        self._deps(e, reads, writes)
        slot = self.di % NDMA
        self.di += 1
        if self.duse[slot] > 0:
            self._wait(e, Ev(("dma", slot), self.duse[slot], self.dsem[slot], 16 * self.duse[slot]))
        inst = fn(e.h)
        self.duse[slot] += 1
        inst.then_inc(self.dsem[slot], 16)
        ev = Ev(("dma", slot), self.duse[slot], self.dsem[slot], 16 * self.duse[slot], dict(e.waited))
        for b in reads:
            b.r[ev.key] = ev
        for b in writes:
            b.w = ev
            b.r = {}
        self.dma_evs.append(ev)
        return ev

    def barrier(self):
        evs = [Ev(x.name, x.count, x.sem, x.count) for x in self.e.values() if x.count > 0]
        devs = [Ev(("dma", s), self.duse[s], self.dsem[s], 16 * self.duse[s]) for s in range(NDMA) if self.duse[s] > 0]
        for e in self.e.values():
            for ev in evs + devs:
                if ev.key != e.name:
                    self._wait(e, ev)
                elif not e.inorder:
                    self._wait(e, ev)

    def finish(self):
        e = self.e["sp"]
        for s in range(NDMA):
            if self.duse[s] > 0:
                self._wait(e, Ev(("dma", s), self.duse[s], self.dsem[s], 16 * self.duse[s]))
        for x in self.e.values():
            if x.name != "sp" and x.count > 0:
                self._wait(e, Ev(x.name, x.count, x.sem, x.count))


class SB:
    def __init__(self, h, F):
        self.h, self.F = h, F

    def ap(self, off, dims, p0=0, n=128):
        return bass.AP(self.h, p0 * self.F + off, [[self.F, n]] + [list(d) for d in dims])

    def s(self, off, cnt, p0=0, n=128):
        return bass.AP(self.h, p0 * self.F + off, [[self.F, n], [1, cnt]])


def dap(h, off, dims):
    return bass.AP(h, off, [list(d) for d in dims])


def build_program():
    nc = bass.Bass("TRN2", target_bir_lowering=False)
    es = ExitStack()

    def din(name, shape, dt=F32):
        return nc.dram_tensor(name, list(shape), dt, kind="ExternalInput")

    def dout(name, shape, dt=F32):
        return nc.dram_tensor(name, list(shape), dt, kind="ExternalOutput")

    def dscr(name, shape, dt):
        return nc.dram_tensor(name, list(shape), dt)

    x_parts = [din("x_h", [OWN, D]), din("x_o", [OWN, D])]
    w_in_d = din("w_in", [D, INW])
    wpr_d = din("w_proj_ret", [1024, 1024])
    wpa_d = din("w_proj_att", [512, 1024])
    wo_d = din("w_out", [1024, 1024])
    gT_d = din("gT", [128, 8])
    rnT_d = din("rnT", [128, 8])
    gq_d = din("gq_tab", [128, 64])
    gk_d = din("gk_tab", [128, 64])
    bt_d = din("bt", [128, 24 * 256])
    tmask_d = din("tmask", [128, 256])
    cos_d = din("cos", [4 * OWN, 64])
    sin_d = din("sin", [4 * OWN, 64])
    dec_d = din("dec", [128, 16])
    cmask_d = din("cmask", [128, 128])
    ident_d = din("ident", [128, 128])
    hv_d = din("hv", [128, 64])
    xpp_d = din("x_pp", [3 * OWN, D])
    xs_d = din("x_s", [128, D])
    coss_d = din("cos_s", [128, 64])
    sins_d = din("sin_s", [128, 64])
    decs_d = din("dec_s", [128, 16])
    smask_d = din("smask", [128, 128])
    colmask_d = din("colmask", [128, 512])
    bts_d = din("bts", [128, 432])
    vms_d = din("vms", [128, 432])
    cch_d = [din("c128", [4 * 128, 2, 512]), din("c512", [4 * 512, 2, 512]), din("c2048", [4 * 2048, 2, 512])]
    s0_d = din("s0", [16, 128, 256])

    y_d = dout("y", [OWN, D])
    sfin_d = dout("sfin", [512, 256])
    kvs_d = [dout("kvs%d" % g, [128, 2, 512]) for g in range(3)]
    rss_d = dout("rss", [16, 128, 256])
    ys_d = dout("ys", [16, D])
    kv128_d = dout("kv128", [128, 2, 512])
    kv512_d = dout("kv512", [512, 2, 512])
    kv2048_d = dout("kv2048", [2048, 2, 512])

    wbf_d = dscr("wbf", [20, 128, 8 * 512], BF16)
    wprb_d = dscr("wprb", [2, 128, 8 * 512], BF16)
    wpab_d = dscr("wpab", [2, 128, 4 * 512], BF16)
    wob_d = dscr("wob", [2, 128, 8 * 512], BF16)
    vscr1_d = dscr("vscr1", [2, 512, 512], BF16)
    vscr2_d = dscr("vscr2", [2, 2048, 512], BF16)
    att_scr_d = dscr("att_scr", [16, 512], F32)

    T = Tracker(nc, es)
    sbuf_used = [0]

    def sb(name, F, dt=F32):
        h = es.enter_context(nc.sbuf_tensor("sb_" + name, [128, F], dt))
        sbuf_used[0] += F * (4 if dt == F32 else 2)
        return SB(h, F)

    def ps(name, F, dt=F32):
        return SB(es.enter_context(nc.psum_tensor(name, [128, F], dt)), F)

    ps_mm = [ps("ps_mm0", 512), ps("ps_mm1", 512)]
    ps_tr = ps("ps_tr", 1024, BF16)
    ps_st = ps("ps_st", 512)
    ps_num = ps("ps_num", 512)
    ps_den = ps("ps_den", 512)
    ps_ret = ps("ps_ret", 512)
    ps_misc = ps("ps_misc", 512)
    B_mm = [Buf("mm0", True), Buf("mm1", True)]
    B_tr = Buf("tr", True)
    _bst = Buf("st", True)
    B_st = [_bst, _bst]
    _bret = Buf("ret", True)
    B_num, B_den, B_ret_sc, B_ret_o, B_misc = Buf("num", True), Buf("den", True), _bret, _bret, Buf("misc", True)
    ps_trs = [ps_tr, SB(ps_misc.h.bitcast(BF16), 1024)]
    B_trs = [B_tr, B_misc]
    ps_mm += [ps_st, ps_num, ps_den]
    B_mm += [B_st[0], B_num, B_den]

    ident_f = sb("ident_f", 128)
    ident_b = sb("ident_b", 128, BF16)
    cmask = sb("cmask", 128)
    gq_tab = sb("gq_tab", 64)
    gk_tab = sb("gk_tab", 64)
    dec = sb("dec", 16)
    gT = sb("gT", 8)
    rnT = sb("rnT", 8)
    hv_f = sb("hv_f", 64)
    hvb = sb("hvb", 64, BF16)
    ones64 = sb("ones64", 64, BF16)
    EB = sb("EB", 24 * 256, BF16)
    B_const = Buf("const")
    B_EB = Buf("EB")

    mult, add, sub = ALU.mult, ALU.add, ALU.subtract

    def ld(dst, src_h, F, q="sp", buf=B_const):
        T.dma(q, lambda e: e.dma_start(out=dst.s(0, F), in_=dap(src_h, 0, [[F, 128], [1, F]])), writes=[buf])

    ld(gT, gT_d, 8)
    ld(rnT, rnT_d, 8)
    with ExitStack() as es0:
        wst = [SB(es0.enter_context(nc.sbuf_tensor("wst%d" % i, [128, 8 * 512], F32)), 4096) for i in range(2)]
        wcv = [SB(es0.enter_context(nc.sbuf_tensor("wcv%d" % i, [128, 8 * 512], BF16)), 4096) for i in range(2)]
        B_wst = [Buf(), Buf()]
        B_wcv = [Buf(), Buf()]
        B_wscr = Buf("wscr")
        jobs = []
        for c in range(20):
            jobs.append((w_in_d, INW, c * 512, 8, gT, wbf_d, c))
        for c in range(2):
            jobs.append((wpr_d, 1024, c * 512, 8, rnT, wprb_d, c))
        for c in range(2):
            jobs.append((wpa_d, 1024, c * 512, 4, None, wpab_d, c))
        for c in range(2):
            jobs.append((wo_d, 1024, c * 512, 8, None, wob_d, c))
        engs = ["dve", "act"]
        ei = 0
        def cv_load(ji):
            src, rowlen, col0, nkt, sc, dst, c = jobs[ji]
            b = ji % 2
            T.dma("sp", lambda e: e.dma_start(
                out=wst[b].ap(0, [[512, nkt], [1, 512]]),
                in_=dap(src, col0, [[rowlen, 128], [128 * rowlen, nkt], [1, 512]])), writes=[B_wst[b]])
        cv_load(0)
        for ji, (src, rowlen, col0, nkt, sc, dst, c) in enumerate(jobs):
            b = ji % 2
            if ji + 1 < len(jobs):
                cv_load(ji + 1)
            for kt in range(nkt):
                en = engs[ei % 2]
                ei += 1
                o_ap = wcv[b].s(kt * 512, 512)
                i_ap = wst[b].s(kt * 512, 512)
                if sc is None:
                    if en == "act":
                        T.op(en, lambda e, o_ap=o_ap, i_ap=i_ap: e.activation(out=o_ap, in_=i_ap, func=AF.Copy),
                             reads=[B_wst[b]], writes=[B_wcv[b]])
                    else:
                        T.op(en, lambda e, o_ap=o_ap, i_ap=i_ap: e.tensor_copy(out=o_ap, in_=i_ap), reads=[B_wst[b]], writes=[B_wcv[b]])
                else:
                    s_ap = sc.s(kt, 1)
                    if en == "act":
                        T.op(en, lambda e, o_ap=o_ap, i_ap=i_ap, s_ap=s_ap: e.activation(out=o_ap, in_=i_ap, func=AF.Copy, scale=s_ap),
                             reads=[B_wst[b], B_const], writes=[B_wcv[b]])
                    else:
                        T.op(en, lambda e, o_ap=o_ap, i_ap=i_ap, s_ap=s_ap: e.tensor_scalar(out=o_ap, in0=i_ap, scalar1=s_ap, scalar2=None, op0=mult),
                             reads=[B_wst[b], B_const], writes=[B_wcv[b]])
            T.dma("sp", lambda e, b=b, dst=dst, c=c, nkt=nkt: e.dma_start(
                out=dap(dst, c * 128 * nkt * 512, [[nkt * 512, 128], [1, nkt * 512]]), in_=wcv[b].s(0, nkt * 512)),
                reads=[B_wcv[b]], writes=[B_wscr])
        T.barrier()
    B_wscr = Buf("wscr_done")

    wch = [sb("wch%d" % i, 8 * 512, BF16) for i in range(2)]
    B_wch = [Buf("wch%d" % i) for i in range(2)]
    xin = [sb("xin%d" % i, 1024) for i in range(2)]
    B_xin = [Buf("xin0"), Buf("xin1")]
    xb = sb("xb", 1024, BF16)
    B_xb = Buf("xb")
    xT = sb("xT", 8 * 512, BF16)
    B_xT = Buf("xT")
    rstd = sb("rstd", 4)
    B_rstd = Buf("rstd")
    cosw = sb("cosw", 256)
    sinw = sb("sinw", 256)
    B_cs = Buf("cs")

    kT0 = sb("kT0", 4 * 640, BF16)
    V0 = sb("V0", 5 * 512, BF16)
    kT1 = sb("kT1", 4 * 1024, BF16)
    V1 = sb("V1", 2 * 4 * 512, BF16)
    kT2 = sb("kT2", 4 * 4096, BF16)
    V2s = [sb("V2s0", 4096, BF16), sb("V2s1", 4096, BF16)]
    B_kT0 = [Buf("kT0_%d" % i) for i in range(5)]
    B_V0 = [Buf("V0_%d" % i) for i in range(5)]
    B_kT1 = [Buf("kT1_0"), Buf("kT1_1")]
    B_V1 = [Buf("V1_0"), Buf("V1_1")]
    B_kT2 = [Buf("kT2_%d" % i) for i in range(8)]
    B_V2s = [Buf("V2s0"), Buf("V2s1")]
    B_vscr1 = [Buf("vscr1_0"), Buf("vscr1_1")]
    B_vscr2 = [Buf("vscr2_%d" % i) for i in range(8)]
    B_rstd_d = Buf("rstd_d")

    S = sb("S", 1024)
    Sbf = sb("Sbf", 1024, BF16)
    B_S = [Buf("S%d" % h) for h in range(4)]
    B_Sbf = [Buf("Sbf%d" % h) for h in range(4)]
    A = sb("A", 4 * 1024, BF16)
    B_A = Buf("A")
    sigb = sb("sigb", 4 * 512, BF16)
    B_sig = Buf("sig")

    U1 = sb("U1", 14336, BF16)
    B_seg = [Buf("seg%d" % i) for i in range(5)]
    O_qT, O_kTr, O_ktok, O_rv, O_on = 0, 2048, 4096, 6144, 10240
    O_gretT = 0
    O_aqT, O_sagT, O_gatT, O_mrg, O_mT = 0, 6144, 8192, 10240, 0

    sq = sb("sq", 1024)
    B_sqh = [Buf("sq0"), Buf("sq1")]
    B_sq = B_sqh[0]
    B_smh = [Buf("smh0"), Buf("smh1")]
    kf = [sb("kf0", 512), sb("kf1", 512)]
    B_kf = [Buf("kf0"), Buf("kf1")]
    kb = [sb("kb%d" % i, 512, BF16) for i in range(6)]
    B_kb = [Buf("kb%d" % i) for i in range(6)]
    small = sb("small", 64)
    B_small = Buf("small")
    smh = sb("smh", 16)
    epsc = sb("epsc", 1)
    pexp = [sb("pexp0", 256, BF16), sb("pexp1", 256, BF16)]
    B_pexp = [Buf("pexp0"), Buf("pexp1")]
    vst = [sb("vst0", 512, BF16), sb("vst1", 512, BF16)]
    B_vst = [Buf("vst0"), Buf("vst1")]
    of32 = [sb("of0", 512), sb("of1", 512)]
    B_of = [Buf("of0"), Buf("of1")]
    Dbf = [sb("Dbf0", 128, BF16), sb("Dbf1", 128, BF16)]
    B_D = [Buf("D0"), Buf("D1")]
    rot = sb("rot", 1024)
    B_rot = Buf("rot")
    rot2 = sb("rot2", 512)
    B_rot2 = Buf("rot2")
    gst, B_gst = rot, B_rot
    Rb, B_Rb = kf[0], B_kf[0]
    tmpa, B_tmpa = kf[1], B_kf[1]

    ctr = {"mm": 0, "kf": 0, "kb": 0, "vst": 0, "of": 0, "D": 0, "st": 0, "pexp": 0, "xin": 0, "v2s": 0, "hr": 0, "mm5": 0, "trb": 0}

    def nxt(k, n=2):
        v = ctr[k] % n
        ctr[k] += 1
        return v


    ld(ident_f, ident_d, 128)
    ld(cmask, cmask_d, 128)
    ld(gq_tab, gq_d, 64)
    ld(gk_tab, gk_d, 64)
    ld(dec, dec_d, 16)
    ld(hv_f, hv_d, 64)
    T.op("dve", lambda e: e.tensor_copy(out=ident_b.s(0, 128), in_=ident_f.s(0, 128)), reads=[B_const], writes=[B_const])
    T.op("dve", lambda e: e.tensor_copy(out=hvb.s(0, 64), in_=hv_f.s(0, 64)), reads=[B_const], writes=[B_const])
    T.op("dve", lambda e: e.memset(ones64.s(0, 64), 1.0), writes=[B_const])
    T.op("dve", lambda e: e.memset(epsc.s(0, 1), EPS), writes=[B_const])
    T.op("dve", lambda e: e.tensor_scalar(out=gq_tab.s(0, 64), in0=gq_tab.s(0, 64), scalar1=0.125, scalar2=None, op0=mult),
         reads=[B_const], writes=[B_const])
    T.op("pool", lambda e: e.memset(S.s(0, 1024), 0.0), writes=B_S)
    for tns, F_, bl in [(kT0, 2560, B_kT0), (V0, 2560, B_V0), (kT1, 4096, B_kT1), (V1, 4096, B_V1)]:
        T.op("pool", lambda e, tns=tns, F_=F_: e.memset(tns.s(0, F_), 0.0), writes=bl)
    for i in range(4):
        T.op("pool", lambda e, i=i: e.memset(kT2.s(i * 4096, 4096), 0.0), writes=B_kT2)
    for i in range(2):
        T.op("pool", lambda e, i=i: e.memset(V2s[i].s(0, 4096), 0.0), writes=[B_V2s[i]])

    tmk = sb("tmk", 256)
    ld(tmk, tmask_d, 256)
    for i in range(12):
        j = nxt("kf")
        T.dma("sp", lambda e, i=i, j=j: e.dma_start(out=kf[j].s(0, 512), in_=dap(bt_d, i * 512, [[24 * 256, 128], [1, 512]])),
              writes=[B_kf[j]])
        T.op("act", lambda e, j=j: e.activation(out=kf[j].s(0, 512), in_=kf[j].s(0, 512), func=AF.Exp),
             reads=[B_kf[j]], writes=[B_kf[j]])
        T.op("dve", lambda e, i=i, j=j: e.tensor_tensor(out=EB.ap(i * 512, [[256, 2], [1, 256]]), in0=kf[j].ap(0, [[256, 2], [1, 256]]),
                                                     in1=tmk.ap(0, [[0, 2], [1, 256]]), op=mult),
             reads=[B_kf[j], B_const], writes=[B_EB])

    def window_prep(row0, widx_scr, cs_row0=None, src=None, ntiles=4):
        for t in range(ntiles):
            j = nxt("xin")
            src_h = x_parts[row0 // OWN] if src is None else src
            T.dma("sp", lambda e, j=j, t=t: e.dma_start(out=xin[j].s(0, 1024), in_=dap(src_h, ((row0 % OWN if src is None else row0) + t * 128) * D, [[D, 128], [1, D]])),
                  writes=[B_xin[j]])
            T.op("act", lambda e, j=j: e.activation(out=sq.s(0, 1024), in_=xin[j].s(0, 1024), func=AF.Square),
                 reads=[B_xin[j]], writes=B_sqh)
            T.op("dve", lambda e, t=t: e.reduce_sum(out=rstd.s(t, 1), in_=sq.s(0, 1024), axis=AX.X), reads=B_sqh, writes=[B_rstd])
            T.op("act", lambda e, j=j: e.activation(out=xb.s(0, 1024), in_=xin[j].s(0, 1024), func=AF.Copy),
                 reads=[B_xin[j]], writes=[B_xb])
            for kt in range(8):
                T.op("pe", lambda e, kt=kt: e.transpose(out=ps_tr.s(kt * 128, 128), in_=xb.s(kt * 128, 128), identity=ident_b.s(0, 128)),
                     reads=[B_xb, B_const], writes=[B_tr])
            T.op("dve", lambda e, t=t: e.tensor_copy(out=xT.ap(t * 128, [[512, 8], [1, 128]]), in_=ps_tr.ap(0, [[128, 8], [1, 128]])),
                 reads=[B_tr], writes=[B_xT])
        T.op("dve", lambda e: e.tensor_scalar(out=rstd.s(0, 4), in0=rstd.s(0, 4), scalar1=1.0 / D, scalar2=EPS, op0=mult, op1=add),
             reads=[B_rstd], writes=[B_rstd])
        T.op("act", lambda e: e.activation(out=rstd.s(0, 4), in_=rstd.s(0, 4), func=AF.Sqrt), reads=[B_rstd], writes=[B_rstd])
        T.op("dve", lambda e: e.reciprocal(out=rstd.s(0, 4), in_=rstd.s(0, 4)), reads=[B_rstd], writes=[B_rstd])
        if cs_row0 is not None:
            T.dma("sp", lambda e: e.dma_start(out=cosw.ap(0, [[64, 4], [1, 64]]), in_=dap(cos_d, cs_row0 * 64, [[64, 128], [128 * 64, 4], [1, 64]])),
                  writes=[B_cs])
            T.dma("sp", lambda e: e.dma_start(out=sinw.ap(0, [[64, 4], [1, 64]]), in_=dap(sin_d, cs_row0 * 64, [[64, 128], [128 * 64, 4], [1, 64]])),
                  writes=[B_cs])

    wl_state = {"i": 0}

    def run_loads(loads, body):
        n = len(loads)
        base = wl_state["i"]

        def issue(i):
            h, c, nkt = loads[i]
            b = (base + i) % 2
            T.dma("sp", lambda e: e.dma_start(out=wch[b].s(0, nkt * 512), in_=dap(h, c * 128 * nkt * 512, [[nkt * 512, 128], [1, nkt * 512]])),
                  reads=[B_wscr], writes=[B_wch[b]])

        issue(0)
        for i in range(n):
            b = (base + i) % 2
            if i + 1 < n:
                issue(i + 1)
            body(i, wch[b], B_wch[b])
        wl_state["i"] = base + n
        flush_pending()

    pending = []
    tile_no = [0]

    def flush_pending(upto=None):
        while pending and (upto is None or pending[0][0] <= upto):
            pending.pop(0)[1]()

    def inproj_tile(wb, Bw, t, nkt=8, lhs=None, Blhs=None):
        if lhs is not None:
            flush_pending()
        m = nxt("mm5", 5)
        for kt in range(nkt):
            l_ap = xT.s(kt * 512 + t * 128, 128) if lhs is None else lhs(kt)
            T.op("pe", lambda e, kt=kt, l_ap=l_ap: e.matmul(ps_mm[m].s(0, 512), l_ap, wb.s(kt * 512, 512), start=(kt == 0), stop=(kt == nkt - 1)),
                 reads=([B_xT] if Blhs is None else Blhs) + [Bw], writes=[B_mm[m]])
        tile_no[0] += 1
        flush_pending(tile_no[0] - 3)
        return ps_mm[m], B_mm[m]

    def transposes_to(src, Bsrc, src_off, n, dst_ap, Bdst, eng="act", scale=None):
        def emit():
            jt = nxt("trb")
            pt_, Bt_ = ps_trs[jt], B_trs[jt]
            for j in range(n):
                T.op("pe", lambda e, j=j: e.transpose(out=pt_.s(j * 128, 128), in_=src.s(src_off + j * 128, 128), identity=ident_b.s(0, 128)),
                     reads=[Bsrc, B_const], writes=[Bt_])
            if eng == "act":
                T.op("act", lambda e: e.activation(out=dst_ap, in_=pt_.ap(0, [[128, n], [1, 128]]), func=AF.Copy), reads=[Bt_], writes=Bdst)
            else:
                T.op(eng, lambda e: e.tensor_copy(out=dst_ap, in_=pt_.ap(0, [[128, n], [1, 128]])), reads=[Bt_], writes=Bdst)
        pending.append((tile_no[0], emit))

    def head_rms(pm, Bpm, t, gtab, want_f32_out):
        hp = nxt("hr")
        so, co = hp * 512, 8 * hp
        Bq, Bs = B_sqh[hp], B_smh[hp]
        T.op("act", lambda e: e.activation(out=sq.s(so, 512), in_=pm.s(0, 512), func=AF.Square, scale=rstd.s(t, 1)),
             reads=[Bpm, B_rstd], writes=[Bq])
        T.op("dve", lambda e: e.reduce_sum(out=smh.s(co, 8), in_=sq.ap(so, [[64, 8], [1, 64]]), axis=AX.X), reads=[Bq], writes=[Bs])
        T.op("act", lambda e: e.activation(out=smh.s(co, 8), in_=smh.s(co, 8), func=AF.Sqrt, scale=1.0 / 64, bias=epsc.s(0, 1)),
             reads=[Bs, B_const], writes=[Bs])
        T.op("dve", lambda e: e.reciprocal(out=smh.s(co, 8), in_=smh.s(co, 8)), reads=[Bs], writes=[Bs])
        jf = nxt("kf")
        T.op("dve", lambda e: e.scalar_tensor_tensor(out=kf[jf].ap(0, [[64, 8], [1, 64]]), in0=pm.ap(0, [[64, 8], [1, 64]]), scalar=rstd.s(t, 1),
                                                     in1=smh.ap(co, [[1, 8], [0, 64]]), op0=mult, op1=mult),
             reads=[Bpm, Bs, B_rstd], writes=[B_kf[jf]])
        jb = nxt("kb", 6)
        T.op("dve", lambda e: e.tensor_tensor(out=kb[jb].ap(0, [[64, 8], [1, 64]]), in0=kf[jf].ap(0, [[64, 8], [1, 64]]), in1=gtab.ap(0, [[0, 8], [1, 64]]), op=mult),
             reads=[B_kf[jf], B_const], writes=[B_kb[jb]])
        jo = None
        if want_f32_out:
            jo = nxt("of")
            T.op("dve", lambda e: e.tensor_tensor(out=of32[jo].ap(0, [[64, 8], [1, 64]]), in0=kf[jf].ap(0, [[64, 8], [1, 64]]), in1=gtab.ap(0, [[0, 8], [1, 64]]), op=mult),
                 reads=[B_kf[jf], B_const], writes=[B_of[jo]])
        return jb, jo

    def v_epilogue(pm, Bpm, t, dst_ap, Bdst, want_f32_out):
        T.op("act", lambda e: e.activation(out=dst_ap, in_=pm.s(0, 512), func=AF.Copy, scale=rstd.s(t, 1)),
             reads=[Bpm, B_rstd], writes=Bdst)
        jo = None
        if want_f32_out:
            jo = nxt("of")
            T.op("dve", lambda e: e.tensor_scalar(out=of32[jo].s(0, 512), in0=pm.s(0, 512), scalar1=rstd.s(t, 1), scalar2=None, op0=mult),
                 reads=[Bpm, B_rstd], writes=[B_of[jo]])
        return jo

    def kv_store(jo, dst_h, row0, kvi, stride_rows=1):
        if "nokvst" in DBG:
            return
        T.dma(QST, lambda e: e.dma_start(out=dap(dst_h, (row0 * 2 + kvi) * 512, [[1024 * stride_rows, 128], [1, 512]]), in_=of32[jo].s(0, 512)),
              reads=[B_of[jo]], writes=[Buf()])

    def rotary(pm, Bpm, t, dec_off, dst_ap, Bdst, dec_sb=None):
        dec_sb = dec if dec_sb is None else dec_sb
        T.op("dve", lambda e: e.tensor_scalar(out=small.s(16, 4), in0=dec_sb.s(dec_off, 4), scalar1=rstd.s(t, 1), scalar2=None, op0=mult),
             reads=[B_const, B_rstd], writes=[B_small])
        c_ap = lambda nrep: cosw.ap(t * 64, [[0, nrep], [1, 64]])
        s_ap = lambda nrep: sinw.ap(t * 64, [[0, nrep], [1, 64]])
        T.op("dve", lambda e: e.tensor_tensor(out=rot.ap(0, [[64, 4], [1, 64]]), in0=pm.ap(64, [[128, 4], [1, 64]]), in1=s_ap(4), op=mult),
             reads=[Bpm, B_cs], writes=[B_rot])
        T.op("dve", lambda e: e.tensor_tensor(out=rot.ap(256, [[64, 4], [1, 64]]), in0=pm.ap(0, [[128, 4], [1, 64]]), in1=s_ap(4), op=mult),
             reads=[Bpm, B_cs], writes=[B_rot])
        T.op("dve", lambda e: e.tensor_tensor(out=rot.ap(512, [[64, 8], [1, 64]]), in0=pm.ap(0, [[64, 8], [1, 64]]), in1=c_ap(8), op=mult),
             reads=[Bpm, B_cs], writes=[B_rot])
        T.op("dve", lambda e: e.tensor_tensor(out=rot2.ap(0, [[128, 4], [1, 64]]), in0=rot.ap(512, [[128, 4], [1, 64]]),
                                               in1=rot.ap(0, [[64, 4], [1, 64]]), op=sub), reads=[B_rot], writes=[B_rot2])
        T.op("dve", lambda e: e.tensor_tensor(out=rot2.ap(64, [[128, 4], [1, 64]]), in0=rot.ap(512 + 64, [[128, 4], [1, 64]]),
                                               in1=rot.ap(256, [[64, 4], [1, 64]]), op=add), reads=[B_rot], writes=[B_rot2])
        T.op("dve", lambda e: e.tensor_tensor(out=dst_ap, in0=rot2.ap(0, [[128, 4], [1, 128]]), in1=small.ap(16, [[1, 4], [0, 128]]), op=mult),
             reads=[B_rot2, B_small], writes=Bdst)

    def state_update_pre(t, h, need_bf):
        T.op("pe", lambda e: e.matmul(ps_misc.s(0, 256), U1.s(O_ktok + t * 512 + h * 128, 128), U1.s(O_rv + t * 1024 + h * 256, 256),
                                      start=True, stop=True), reads=[B_seg[2], B_seg[3]], writes=[B_misc])
        T.op("dve", lambda e: e.scalar_tensor_tensor(out=S.s(h * 256, 256), in0=S.s(h * 256, 256), scalar=dec.s(8 + h, 1), in1=ps_misc.s(0, 256),
                                                     op0=mult, op1=add), reads=[B_misc, B_S[h], B_const], writes=[B_S[h]])
        if need_bf:
            T.op("act", lambda e: e.activation(out=Sbf.s(h * 256, 256), in_=S.s(h * 256, 256), func=AF.Copy), reads=[B_S[h]], writes=[B_Sbf[h]])

    def state_update(t, h, need_bf):
        T.op("pe", lambda e: e.matmul(ps_misc.s(0, 256), U1.s(O_ktok + t * 512 + h * 128, 128), U1.s(O_rv + t * 1024 + h * 256, 256),
                                      start=True, stop=True), reads=[B_seg[2], B_seg[3]], writes=[B_misc])
        T.op("dve", lambda e: e.tensor_scalar(out=S.s(h * 256, 256), in0=S.s(h * 256, 256), scalar1=dec.s(8 + h, 1), scalar2=None, op0=mult),
             reads=[B_S[h], B_const], writes=[B_S[h]])
        T.op("dve", lambda e: e.scalar_tensor_tensor(out=S.s(h * 256, 256), in0=ps_misc.s(0, 256), scalar=dec.s(8 + h, 1), in1=S.s(h * 256, 256),
                                                     op0=mult, op1=add), reads=[B_misc, B_S[h], B_const], writes=[B_S[h]])
        if need_bf:
            T.op("act", lambda e: e.activation(out=Sbf.s(h * 256, 256), in_=S.s(h * 256, 256), func=AF.Copy), reads=[B_S[h]], writes=[B_Sbf[h]])

    def rk_rv_body(i, wb, Bw):
        for t in range(4):
            pm, Bpm = inproj_tile(wb, Bw, t)
            if i == 0:
                rotary(pm, Bpm, t, 12, U1.ap(O_ktok + t * 512, [[128, 4], [1, 128]]), [B_seg[2]])
            else:
                v_epilogue(pm, Bpm, t, U1.s(O_rv + t * 1024 + (i - 1) * 512, 512), [B_seg[3]], False)

    if STAGE >= 2:
        for w in range(12):
            window_prep(w * 512, 0, cs_row0=w * 512, src=xpp_d)
            run_loads([(wbf_d, 1, 8), (wbf_d, 2, 8), (wbf_d, 3, 8)], rk_rv_body)
            for t in range(4):
                for h in range(4):
                    state_update_pre(t, h, w == 11 and t == 3)

    def k_chunk(g, gw, wb, Bw):
        own = gw >= 4
        for t in range(4):
            pm, Bpm = inproj_tile(wb, Bw, t)
            want = own and ((g == 2) or (g == 1 and gw == 7 and "nokv512" not in DBG) or (g == 0 and gw == 7 and t == 3 and "nokv128" not in DBG))
            jb, jo = head_rms(pm, Bpm, t, gk_tab, want)
            if g == 0:
                slot = (gw * 4 + t) % 5
                dst, Bd = kT0.ap(slot * 128, [[640, 4], [1, 128]]), [B_kT0[slot]]
            elif g == 1:
                dst, Bd = kT1.ap((gw % 2) * 512 + t * 128, [[1024, 4], [1, 128]]), [B_kT1[gw % 2]]
            else:
                gwx = 4 if "kt2fix" in DBG else gw
                dst, Bd = kT2.ap(gwx * 512 + t * 128, [[4096, 4], [1, 128]]), [B_kT2[gw]]
            transposes_to(kb[jb], B_kb[jb], 0, 4, dst, Bd)
            if want:
                if g == 2:
                    kv_store(jo, kv2048_d, (gw - 4) * 512 + t * 128, 0)
                elif g == 1:
                    kv_store(jo, kv512_d, t * 128, 0)
                else:
                    kv_store(jo, kv128_d, 0, 0)

    def v_chunk(g, gw, wb, Bw):
        own = gw >= 4
        for t in range(4):
            pm, Bpm = inproj_tile(wb, Bw, t)
            want = own and ((g == 2) or (g == 1 and gw == 7 and "nokv512" not in DBG) or (g == 0 and gw == 7 and t == 3 and "nokv128" not in DBG))
            if g == 0:
                slot = (gw * 4 + t) % 5
                jo = v_epilogue(pm, Bpm, t, V0.s(slot * 512, 512), [B_V0[slot]], want)
            else:
                jv = nxt("vst")
                jo = v_epilogue(pm, Bpm, t, vst[jv].s(0, 512), [B_vst[jv]], want)
                if g == 1:
                    T.dma(QST, lambda e, jv=jv, t=t: e.dma_start(out=dap(vscr1_d, ((gw % 2) * 512 + t * 128) * 512, [[512, 128], [1, 512]]),
                                                                    in_=vst[jv].s(0, 512)), reads=[B_vst[jv]], writes=[B_vscr1[gw % 2]])
                elif "novscr2" not in DBG:
                    T.dma(QST, lambda e, jv=jv, t=t: e.dma_start(out=dap(vscr2_d, (gw * 512 + t * 128) * 512, [[512, 128], [1, 512]]),
                                                                    in_=vst[jv].s(0, 512)), reads=[B_vst[jv]], writes=[B_vscr2[gw]])
            if want:
                if g == 2:
                    kv_store(jo, kv2048_d, (gw - 4) * 512 + t * 128, 1)
                elif g == 1:
                    kv_store(jo, kv512_d, t * 128, 1)
                else:
                    kv_store(jo, kv128_d, 0, 1)
        if g == 1:
            sl = gw % 2
            T.dma(QST, lambda e: e.dma_start(out=V1.s(sl * 2048, 2048), in_=dap(vscr1_d, sl * 512 * 512, [[2048, 128], [1, 2048]])),
                  reads=[B_vscr1[sl]], writes=[B_V1[sl]])

    if STAGE >= 1:
        for gw in range(4):
            if "only3" in DBG and gw != 3:
                continue
            if "own4" in DBG or "own7" in DBG or "own5" in DBG:
                continue
            if "two" in DBG and gw < 2:
                continue
            if "first2" in DBG and gw >= 2:
                continue
            if "three" in DBG and gw < 1:
                continue
            if "bar" in DBG:
                T.barrier()
            window_prep(gw * 512, gw)
            if "preponly" in DBG:
                continue
            loads = [(wbf_d, 11, 8), (wbf_d, 14, 8)]
            kinds = [("k", 2), ("v", 2)]
            if gw == 3:
                loads += [(wbf_d, 9, 8), (wbf_d, 10, 8), (wbf_d, 12, 8), (wbf_d, 13, 8)]
                kinds += [("k", 0), ("k", 1), ("v", 0), ("v", 1)]

            def body(i, wb, Bw, kinds=kinds, gw=gw):
                kd, g = kinds[i]
                (k_chunk if kd == "k" else v_chunk)(g, gw, wb, Bw)
            run_loads(loads, body)

    if STAGE in (1, 2) and "only3" not in DBG and "haloonly" not in DBG:
        for gw in range(4, 8):
            if "own4" in DBG and gw != 4:
                continue
            if "own7" in DBG and gw != 7:
                continue
            if "own5" in DBG and gw != 5:
                continue
            window_prep(gw * 512, gw)
            kinds = [("k", 0), ("k", 1), ("k", 2), ("v", 0), ("v", 1), ("v", 2)]

            def body(i, wb, Bw, kinds=kinds, gw=gw):
                kd, g = kinds[i]
                (k_chunk if kd == "k" else v_chunk)(g, gw, wb, Bw)
            run_loads([(wbf_d, c, 8) for c in range(9, 15)], body)

    def retention_core(t):
        flush_pending()
        ps_sc, B_sc = [ps_num, ps_den], [B_num, B_den]

        def scores(h):
            qT_h = U1.s(O_qT + h * 512 + t * 128, 128)
            kT_h = U1.s(O_kTr + h * 512 + t * 128, 128)
            T.op("pe", lambda e: e.matmul(ps_sc[h % 2].s(0, 128), kT_h, qT_h, start=True, stop=True), reads=[B_seg[0], B_seg[1]], writes=[B_sc[h % 2]])
            jd = nxt("D")
            T.op("dve", lambda e: e.tensor_tensor(out=Dbf[jd].s(0, 128), in0=ps_sc[h % 2].s(0, 128), in1=cmask.s(0, 128), op=mult),
                 reads=[B_sc[h % 2], B_const], writes=[B_D[jd]])
            return jd
        jds = {0: scores(0)}
        for h in range(4):
            ps_r, B_r = ([ps_ret, ps_st][h % 2], [B_ret_o, B_st[0]][h % 2])
            qT_h = U1.s(O_qT + h * 512 + t * 128, 128)
            v_h = U1.s(O_rv + t * 1024 + h * 256, 256)
            if h + 1 < 4:
                jds[h + 1] = scores(h + 1)
            jd = jds[h]
            T.op("pe", lambda e: e.matmul(ps_r.s(128, 256), Dbf[jd].s(0, 128), v_h, start=True, stop=False, skip_group_check=True),
                 reads=[B_D[jd], B_seg[3]], writes=[B_r])
            T.op("pe", lambda e: e.matmul(ps_r.s(128, 256), qT_h, Sbf.s(h * 256, 256), start=False, stop=True, skip_group_check=True),
                 reads=[B_seg[0], B_Sbf[h]], writes=[B_r])
            state_update(t, h, True)
            gn_tail(U1.s(O_on + t * 1024 + h * 256, 256), ps_r, B_r)

    def gn_tail(dst_ap, ps_r=None, B_r=None):
        ps_r = ps_ret if ps_r is None else ps_r
        B_r = B_ret_o if B_r is None else B_r
        if True:
            o_ap = ps_r.s(128, 256)
            T.op("dve", lambda e: e.reduce_sum(out=small.s(32, 1), in_=o_ap, axis=AX.X), reads=[B_r], writes=[B_small])
            T.op("act", lambda e: e.activation(out=sq.s(0, 256), in_=o_ap, func=AF.Square), reads=[B_r], writes=[B_sq])
            T.op("dve", lambda e: e.reduce_sum(out=small.s(33, 1), in_=sq.s(0, 256), axis=AX.X), reads=[B_sq], writes=[B_small])
            T.op("dve", lambda e: e.tensor_scalar(out=small.s(34, 1), in0=small.s(32, 1), scalar1=1.0 / 256, scalar2=None, op0=mult),
                 reads=[B_small], writes=[B_small])
            T.op("dve", lambda e: e.tensor_tensor(out=small.s(35, 1), in0=small.s(34, 1), in1=small.s(34, 1), op=mult),
                 reads=[B_small], writes=[B_small])
            T.op("dve", lambda e: e.scalar_tensor_tensor(out=small.s(36, 1), in0=small.s(33, 1), scalar=1.0 / 256, in1=small.s(35, 1), op0=mult, op1=sub),
                 reads=[B_small], writes=[B_small])
            T.op("dve", lambda e: e.tensor_scalar(out=small.s(37, 1), in0=small.s(36, 1), scalar1=EPS, scalar2=None, op0=add),
                 reads=[B_small], writes=[B_small])
            T.op("act", lambda e: e.activation(out=small.s(37, 1), in_=small.s(37, 1), func=AF.Sqrt), reads=[B_small], writes=[B_small])
            T.op("dve", lambda e: e.reciprocal(out=small.s(37, 1), in_=small.s(37, 1)), reads=[B_small], writes=[B_small])
            T.op("dve", lambda e: e.scalar_tensor_tensor(out=small.s(38, 1), in0=small.s(34, 1), scalar=-1.0, in1=small.s(37, 1), op0=mult, op1=mult),
                 reads=[B_small], writes=[B_small])
            T.op("act", lambda e: e.activation(out=dst_ap, in_=o_ap, func=AF.Identity,
                                               scale=small.s(37, 1), bias=small.s(38, 1)), reads=[B_r, B_small], writes=[B_seg[4]])

    ps_sts = [ps_st, ps_ret]
    B_sts = [B_st[0], B_ret_sc]

    def attention(w):
        flush_pending()
        gw = 4 + w
        EBo = lambda g, h: (g * 8 + h) * 256
        jv2 = [0]

        def v2_load(pt_):
            jv = nxt("v2s")
            T.dma("sp", lambda e: e.dma_start(out=V2s[jv].ap(0, [[128, 16], [1, 128]]),
                                              in_=dap(vscr2_d, pt_ * 128, [[8192, 128], [512, 16], [1, 128]])),
                  reads=B_vscr2[0:4], writes=[B_V2s[jv]])
            nown = 32 * (w + 1)
            T.dma("sp", lambda e: e.dma_start(out=V2s[jv].ap(2048, [[128, 16], [1, 128]], p0=0, n=nown),
                                              in_=dap(vscr2_d, 2048 * 512 + pt_ * 128, [[8192, nown], [512, 16], [1, 128]])),
                  reads=B_vscr2[4:5 + w], writes=[B_V2s[jv]])
            return jv
        jv_next = [v2_load(0)]
        for h in range(8 if "heads4" not in DBG else 4):
            pt, po = h // 2, (h % 2) * 64
            first = [True, True]
            if h % 2 == 0:
                jv2[0] = jv_next[0]
                if pt + 1 < 4:
                    jv_next[0] = v2_load(pt + 1)

            def pv(rhs_ap, Brhs, v_ap, Bv, ones_ap, out_off, ncol, ostep):
                o_num = ps_num.ap(out_off, [[ostep, ncol]], p0=0, n=64)
                o_den = ps_den.ap(out_off, [[ostep, ncol]], p0=0, n=64)
                T.op("pe", lambda e: e.matmul(o_num, v_ap, rhs_ap, start=first[0], stop=False, skip_group_check=True),
                     reads=[Brhs] + Bv, writes=[B_num])
                first[0] = False
                T.op("pe", lambda e: e.matmul(o_den, ones_ap, rhs_ap, start=first[1], stop=False, skip_group_check=True),
                     reads=[Brhs, B_const], writes=[B_den])
                first[1] = False

            def softmax_piece(js, ncols, eb_ap):
                jp = nxt("pexp")
                T.op("act", lambda e: e.activation(out=pexp[jp].s(0, ncols), in_=ps_sts[js].s(0, ncols), func=AF.Exp),
                     reads=[B_sts[js]], writes=[B_pexp[jp]])
                T.op("dve", lambda e: e.tensor_tensor(out=pexp[jp].s(0, ncols), in0=pexp[jp].s(0, ncols), in1=eb_ap, op=mult),
                     reads=[B_pexp[jp], B_EB], writes=[B_pexp[jp]])
                return jp

            units = []

            def mk_g0(t):
                js, jp = nxt("st"), nxt("pexp")
                q_ap = U1.s(O_aqT + (0 * 4 + pt) * 512 + t * 128, 128, p0=po, n=64)
                cur = (gw * 4 + t) % 5
                prev = (gw * 4 + t - 1) % 5

                def qk():
                    for ki, slot in enumerate([prev, cur]):
                        k_ap = kT0.s(pt * 640 + slot * 128, 128, p0=po, n=64)
                        T.op("pe", lambda e: e.matmul(ps_sts[js].s(ki * 128, 128), k_ap, q_ap, start=True, stop=True, skip_group_check=True),
                             reads=[B_seg[0], B_seg[1], B_seg[2], B_kT0[slot]], writes=[B_sts[js]])

                def sm():
                    T.op("act", lambda e: e.activation(out=pexp[jp].s(0, 256), in_=ps_sts[js].s(0, 256), func=AF.Exp),
                         reads=[B_sts[js]], writes=[B_pexp[jp]])
                    T.op("dve", lambda e: e.tensor_tensor(out=pexp[jp].s(0, 256), in0=pexp[jp].s(0, 256), in1=EB.s(EBo(0, h), 256), op=mult),
                         reads=[B_pexp[jp], B_EB], writes=[B_pexp[jp]])

                def pvf():
                    for ki, slot in enumerate([prev, cur]):
                        halo = (gw == 4 and t == 0 and ki == 0)
                        pv(pexp[jp].s(ki * 128, 128), B_pexp[jp], V0.s(slot * 512 + h * 64, 64), [B_V0[slot]],
                           (hvb if halo else ones64).s(0, 64), t * 128, 128, 1)
                return qk, sm, pvf

            def mk_g1(r):
                js, jp = nxt("st"), nxt("pexp")
                q_ap = U1.ap(O_aqT + (1 * 4 + pt) * 512 + r, [[4, 128]], p0=po, n=64)
                sls = [(gw - 1) % 2, gw % 2]

                def qk():
                    for ki, sl in enumerate(sls):
                        k_ap = kT1.ap(pt * 1024 + sl * 512 + r, [[4, 128]], p0=po, n=64)
                        T.op("pe", lambda e: e.matmul(ps_sts[js].s(ki * 128, 128), k_ap, q_ap, start=True, stop=True, skip_group_check=True),
                             reads=[B_seg[0], B_seg[1], B_seg[2], B_kT1[sl]], writes=[B_sts[js]])

                def sm():
                    T.op("act", lambda e: e.activation(out=pexp[jp].s(0, 256), in_=ps_sts[js].s(0, 256), func=AF.Exp),
                         reads=[B_sts[js]], writes=[B_pexp[jp]])
                    T.op("dve", lambda e: e.tensor_tensor(out=pexp[jp].s(0, 256), in0=pexp[jp].s(0, 256), in1=EB.s(EBo(1, h), 256), op=mult),
                         reads=[B_pexp[jp], B_EB], writes=[B_pexp[jp]])

                def pvf():
                    for ki, sl in enumerate(sls):
                        halo = (gw == 4 and ki == 0)
                        pv(pexp[jp].s(ki * 128, 128), B_pexp[jp], V1.s(sl * 2048 + r * 512 + h * 64, 64), [B_V1[sl]],
                           (hvb if halo else ones64).s(0, 64), r, 128, 4)
                return qk, sm, pvf

            def mk_g2(rb):
                js, jp = nxt("st"), nxt("pexp")

                def qk():
                    for rr in range(4):
                        r = rb * 4 + rr
                        q_ap = U1.ap(O_aqT + (2 * 4 + pt) * 512 + r, [[16, 32]], p0=po, n=64)
                        for half in range(2):
                            k_ap = kT2.ap(pt * 4096 + half * 2048 + r, [[16, 128]], p0=po, n=64)
                            T.op("pe", lambda e: e.matmul(ps_sts[js].s(rr * 64 + half * 32, 32), k_ap, q_ap, start=True, stop=True, skip_group_check=True),
                                 reads=[B_seg[0], B_seg[1], B_seg[2]] + B_kT2, writes=[B_sts[js]])

                def sm():
                    eb_ap = EB.ap(EBo(2, h) + 32 * w, [[0, 4], [128, 2], [1, 32]])
                    T.op("act", lambda e: e.activation(out=pexp[jp].s(0, 256), in_=ps_sts[js].s(0, 256), func=AF.Exp),
                         reads=[B_sts[js]], writes=[B_pexp[jp]])
                    T.op("dve", lambda e: e.tensor_tensor(out=pexp[jp].ap(0, [[64, 4], [32, 2], [1, 32]]), in0=pexp[jp].ap(0, [[64, 4], [32, 2], [1, 32]]),
                                                          in1=eb_ap, op=mult), reads=[B_pexp[jp], B_EB], writes=[B_pexp[jp]])

                def pvf():
                    for rr in range(4):
                        r = rb * 4 + rr
                        for half in range(2):
                            pv(pexp[jp].s(rr * 64 + half * 32, 32), B_pexp[jp], V2s[jv2[0]].s(half * 2048 + r * 128 + (h % 2) * 64, 64), [B_V2s[jv2[0]]],
                               (hvb if half == 0 else ones64).s(0, 64), r, 32, 16)
                return qk, sm, pvf

            for t in range(4):
                units.append(mk_g0(t))
            for r in range(4):
                units.append(mk_g1(r))
            for rb in range(4):
                units.append(mk_g2(rb))
            units[0][0]()
            for k in range(len(units)):
                if k + 1 < len(units):
                    units[k + 1][0]()
                units[k][1]()
                units[k][2]()
            T.op("dve", lambda e: e.reciprocal(out=Rb.s(0, 512, n=64), in_=ps_den.s(0, 512, n=64)), reads=[B_den], writes=[B_Rb])
            T.op("dve", lambda e: e.tensor_tensor(out=tmpa.s(0, 512, n=64), in0=ps_num.s(0, 512, n=64),
                                                  in1=U1.s(O_sagT + pt * 512, 512, p0=po, n=64), op=mult),
                 reads=[B_num, B_seg[3]], writes=[B_tmpa])
            T.op("dve", lambda e: e.tensor_tensor(out=U1.s(O_gatT + pt * 512, 512, p0=po, n=64), in0=tmpa.s(0, 512, n=64),
                                                  in1=Rb.s(0, 512, n=64), op=mult), reads=[B_tmpa, B_Rb], writes=[B_seg[3]])

    if STAGE >= 3:
        for w in range(4):
            gw = 4 + w
            window_prep(OWN + w * 512, gw, cs_row0=3 * OWN + w * 512)
            def retA(i, wb, Bw):
                for t in range(4):
                    pm, Bpm = inproj_tile(wb, Bw, t)
                    if i == 0:
                        jb = nxt("kb", 6)
                        rotary(pm, Bpm, t, 0, kb[jb].ap(0, [[128, 4], [1, 128]]), [B_kb[jb]])
                        transposes_to(kb[jb], B_kb[jb], 0, 4, U1.ap(O_qT + t * 128, [[512, 4], [1, 128]]), [B_seg[0]])
                    elif i == 1:
                        rotary(pm, Bpm, t, 4, U1.ap(O_ktok + t * 512, [[128, 4], [1, 128]]), [B_seg[2]])
                        transposes_to(U1, B_seg[2], O_ktok + t * 512, 4, U1.ap(O_kTr + t * 128, [[512, 4], [1, 128]]), [B_seg[1]])
                    else:
                        v_epilogue(pm, Bpm, t, U1.s(O_rv + t * 1024 + (i - 2) * 512, 512), [B_seg[3]], False)
            run_loads([(wbf_d, c, 8) for c in range(4)], retA)
            for t in range(4):
                retention_core(t)

            def retB(i, wb, Bw):
                if i < 2:
                    for t in range(4):
                        pm, Bpm = inproj_tile(wb, Bw, t)
                        jf = nxt("kf")
                        T.op("act", lambda e: e.activation(out=kf[jf].s(0, 512), in_=pm.s(0, 512), func=AF.Silu, scale=rstd.s(t, 1)),
                             reads=[Bpm, B_rstd], writes=[B_kf[jf]])
                        o_ap = U1.s(O_on + t * 1024 + i * 512, 512)
                        T.op("dve", lambda e: e.tensor_tensor(out=o_ap, in0=o_ap, in1=kf[jf].s(0, 512), op=mult),
                             reads=[B_kf[jf], B_seg[4]], writes=[B_seg[4]])
                    if i == 1:
                        for t in range(4):
                            transposes_to(U1, B_seg[4], O_on + t * 1024, 8, U1.ap(O_gretT + t * 128, [[512, 8], [1, 128]]), [B_seg[0], B_seg[1]])
                elif i in (2, 4):
                    for t in range(4):
                        pm, Bpm = inproj_tile(wb, Bw, t)
                        T.op("act", lambda e: e.activation(out=sigb.s(t * 512, 512), in_=pm.s(0, 512), func=AF.Sigmoid, scale=rstd.s(t, 1)),
                             reads=[Bpm, B_rstd], writes=[B_sig])
                else:
                    j = (i - 3) // 2
                    for t in range(4):
                        pm, Bpm = inproj_tile(wb, Bw, t, 8, lhs=lambda kt: U1.s(O_gretT + kt * 512 + t * 128, 128), Blhs=[B_seg[0], B_seg[1]])
                        T.op("dve", lambda e: e.tensor_tensor(out=A.s(t * 1024 + j * 512, 512), in0=pm.s(0, 512), in1=sigb.s(t * 512, 512), op=mult),
                             reads=[Bpm, B_sig, B_seg[1]], writes=[B_A])
            run_loads([(wbf_d, 4, 8), (wbf_d, 5, 8), (wbf_d, 16, 8), (wprb_d, 0, 8), (wbf_d, 17, 8), (wprb_d, 1, 8)], retB)

            def attA(i, wb, Bw):
                c = 6 + i
                if c < 9:
                    g = c - 6
                    for t in range(4):
                        pm, Bpm = inproj_tile(wb, Bw, t)
                        jb, _ = head_rms(pm, Bpm, t, gq_tab, False)
                        transposes_to(kb[jb], B_kb[jb], 0, 4, U1.ap(O_aqT + g * 4 * 512 + t * 128, [[512, 4], [1, 128]]),
                                      [B_seg[0], B_seg[1], B_seg[2]])
                elif c < 12:
                    k_chunk(c - 9, gw, wb, Bw)
                elif c < 15:
                    v_chunk(c - 12, gw, wb, Bw)
                else:
                    for t in range(4):
                        pm, Bpm = inproj_tile(wb, Bw, t)
                        jb = nxt("kb", 6)
                        T.op("act", lambda e: e.activation(out=kb[jb].s(0, 512), in_=pm.s(0, 512), func=AF.Silu, scale=rstd.s(t, 1)),
                             reads=[Bpm, B_rstd], writes=[B_kb[jb]])
                        transposes_to(kb[jb], B_kb[jb], 0, 4, U1.ap(O_sagT + t * 128, [[512, 4], [1, 128]]), [B_seg[3]])
            run_loads([(wbf_d, c, 8) for c in range(6, 16)], attA)
            if "noatt" not in DBG:
                attention(w)

            def attB(i, wb, Bw):
                if i in (0, 2):
                    for t in range(4):
                        pm, Bpm = inproj_tile(wb, Bw, t)
                        T.op("act", lambda e: e.activation(out=sigb.s(t * 512, 512), in_=pm.s(0, 512), func=AF.Sigmoid, scale=rstd.s(t, 1)),
                             reads=[Bpm, B_rstd], writes=[B_sig])
                elif i in (1, 3):
                    j = (i - 1) // 2
                    for t in range(4):
                        pm, Bpm = inproj_tile(wb, Bw, t, 4, lhs=lambda kt: U1.s(O_gatT + kt * 512 + t * 128, 128), Blhs=[B_seg[3]])
                        jf = nxt("kf")
                        T.op("dve", lambda e: e.tensor_tensor(out=kf[jf].s(0, 512), in0=pm.s(0, 512), in1=sigb.s(t * 512, 512), op=mult),
                             reads=[Bpm, B_sig], writes=[B_kf[jf]])
                        T.op("dve", lambda e: e.tensor_tensor(out=U1.s(O_mrg + t * 1024 + j * 512, 512), in0=kf[jf].s(0, 512),
                                                               in1=A.s(t * 1024 + j * 512, 512), op=add),
                             reads=[B_kf[jf], B_A], writes=[B_seg[4]])
                    if i == 3:
                        for t in range(4):
                            transposes_to(U1, B_seg[4], O_mrg + t * 1024, 8, U1.ap(O_mT + t * 128, [[512, 8], [1, 128]]), [B_seg[0], B_seg[1]])
                else:
                    j = i - 4
                    for t in range(4):
                        pm, Bpm = inproj_tile(wb, Bw, t, 8, lhs=lambda kt: U1.s(O_mT + kt * 512 + t * 128, 128), Blhs=[B_seg[0], B_seg[1]])
                        jx = nxt("xin")
                        T.dma("sp", lambda e: e.dma_start(out=xin[jx].s(0, 512), in_=dap(x_parts[1], (w * 512 + t * 128) * D + j * 512, [[D, 128], [1, 512]])),
                              writes=[B_xin[jx]])
                        jo = nxt("of")
                        T.op("dve", lambda e: e.tensor_tensor(out=of32[jo].s(0, 512), in0=pm.s(0, 512), in1=xin[jx].s(0, 512), op=add),
                             reads=[Bpm, B_xin[jx], B_seg[1]], writes=[B_of[jo]])
                        T.dma(QST, lambda e: e.dma_start(out=dap(y_d, (w * 512 + t * 128) * D + j * 512, [[D, 128], [1, 512]]), in_=of32[jo].s(0, 512)),
                              reads=[B_of[jo]], writes=[Buf()])
            run_loads([(wbf_d, 18, 8), (wpab_d, 0, 4), (wbf_d, 19, 8), (wpab_d, 1, 4), (wob_d, 0, 8), (wob_d, 1, 8)], attB)

    if STAGE >= 2 and "nosample" not in DBG:
        decs = sb("decs", 16)
        kms = [sb("kms0", 128, BF16), sb("kms1", 128, BF16)]
        B_kms = [Buf("kms0"), Buf("kms1")]
        smask_b = sb("smask_b", 128, BF16)
        colmask_b = sb("colmask_b", 512, BF16)
        EBs = sb("EBs", 432)
        onesf = sb("onesf", 1)
        s0b = [sb("s0b0", 256, BF16), sb("s0b1", 256, BF16)]
        B_s0b = [Buf("s0b0"), Buf("s0b1")]
        qm = sb("qm", 512, BF16)
        B_qm = Buf("qm")
        segs012 = [B_seg[0], B_seg[1], B_seg[2]]
        T.dma("sp", lambda e: e.dma_start(out=decs.s(0, 16), in_=dap(decs_d, 0, [[16, 128], [1, 16]])), writes=[B_const])
        j = nxt("kf")
        T.dma("sp", lambda e: e.dma_start(out=kf[j].s(0, 128), in_=dap(smask_d, 0, [[128, 128], [1, 128]])), writes=[B_kf[j]])
        T.op("dve", lambda e: e.tensor_copy(out=smask_b.s(0, 128), in_=kf[j].s(0, 128)), reads=[B_kf[j]], writes=[B_const])
        j = nxt("kf")
        T.dma("sp", lambda e: e.dma_start(out=kf[j].s(0, 512), in_=dap(colmask_d, 0, [[512, 128], [1, 512]])), writes=[B_kf[j]])
        T.op("dve", lambda e: e.tensor_copy(out=colmask_b.s(0, 512), in_=kf[j].s(0, 512)), reads=[B_kf[j]], writes=[B_const])
        j0 = nxt("kf")
        T.dma("sp", lambda e: e.dma_start(out=kf[j0].s(0, 432), in_=dap(bts_d, 0, [[432, 128], [1, 432]])), writes=[B_kf[j0]])
        T.op("act", lambda e: e.activation(out=kf[j0].s(0, 432), in_=kf[j0].s(0, 432), func=AF.Exp), reads=[B_kf[j0]], writes=[B_kf[j0]])
        j1 = nxt("kf")
        T.dma("sp", lambda e: e.dma_start(out=kf[j1].s(0, 432), in_=dap(vms_d, 0, [[432, 128], [1, 432]])), writes=[B_kf[j1]])
        T.op("dve", lambda e: e.tensor_tensor(out=EBs.s(0, 432), in0=kf[j0].s(0, 432), in1=kf[j1].s(0, 432), op=mult),
             reads=[B_kf[j0], B_kf[j1]], writes=[B_const])
        T.op("dve", lambda e: e.memset(onesf.s(0, 1), 1.0), writes=[B_const])
        window_prep(0, 0, src=xs_d, ntiles=1)
        T.dma("sp", lambda e: e.dma_start(out=cosw.s(0, 64), in_=dap(coss_d, 0, [[64, 128], [1, 64]])), writes=[B_cs])
        T.dma("sp", lambda e: e.dma_start(out=sinw.s(0, 64), in_=dap(sins_d, 0, [[64, 128], [1, 64]])), writes=[B_cs])

        def sA(i, wb, Bw):
            pm, Bpm = inproj_tile(wb, Bw, 0)
            if i == 0:
                jb = nxt("kb", 6)
                rotary(pm, Bpm, 0, 12, kb[jb].ap(0, [[128, 4], [1, 128]]), [B_kb[jb]], dec_sb=decs)
                transposes_to(kb[jb], B_kb[jb], 0, 4, U1.ap(O_qT, [[512, 4], [1, 128]]), [B_seg[0]])
            elif i == 1:
                rotary(pm, Bpm, 0, 0, U1.ap(O_ktok, [[128, 4], [1, 128]]), [B_seg[2]], dec_sb=decs)
                transposes_to(U1, B_seg[2], O_ktok, 4, U1.ap(O_kTr, [[512, 4], [1, 128]]), [B_seg[1]])
            else:
                v_epilogue(pm, Bpm, 0, U1.s(O_rv + (i - 2) * 512, 512), [B_seg[3]], False)
        run_loads([(wbf_d, c, 8) for c in range(4)], sA)

        for h in range(4):
            qT_h = U1.s(O_qT + h * 512, 128)
            kT_h = U1.s(O_kTr + h * 512, 128)
            v_h = U1.s(O_rv + h * 256, 256)
            T.op("pe", lambda e: e.matmul(ps_ret.s(0, 128), kT_h, qT_h, start=True, stop=True), reads=[B_seg[0], B_seg[1]], writes=[B_ret_sc])
            jd = nxt("D")
            T.op("dve", lambda e: e.tensor_tensor(out=Dbf[jd].s(0, 128), in0=ps_ret.s(0, 128), in1=smask_b.s(0, 128), op=mult),
                 reads=[B_ret_sc, B_const], writes=[B_D[jd]])
            T.op("dve", lambda e: e.tensor_tensor(out=qm.ap(0, [[128, 4], [1, 128]]), in0=U1.ap(O_qT + h * 512, [[0, 4], [1, 128]]),
                                                  in1=colmask_b.ap(0, [[128, 4], [1, 128]]), op=mult),
                 reads=[B_seg[0], B_const], writes=[B_qm])
            T.op("pe", lambda e: e.matmul(ps_ret.s(128, 256), Dbf[jd].s(0, 128), v_h, start=True, stop=False, skip_group_check=True),
                 reads=[B_D[jd], B_seg[3]], writes=[B_ret_o])
            for bl in range(4):
                jx = nxt("xin")
                T.dma("sp", lambda e: e.dma_start(out=xin[jx].s(0, 256), in_=dap(s0_d, (bl * 4 + h) * 128 * 256, [[256, 128], [1, 256]])),
                      writes=[B_xin[jx]])
                js = bl % 2
                T.op("act", lambda e: e.activation(out=s0b[js].s(0, 256), in_=xin[jx].s(0, 256), func=AF.Copy), reads=[B_xin[jx]], writes=[B_s0b[js]])
                T.op("pe", lambda e: e.matmul(ps_ret.s(128, 256), qm.s(bl * 128, 128), s0b[js].s(0, 256), start=False, stop=(bl == 3), skip_group_check=True),
                     reads=[B_qm, B_s0b[js]], writes=[B_ret_o])
                jk = (bl * 4 + h) % 2
                T.op("dve", lambda e: e.tensor_scalar(out=kms[jk].s(0, 128), in0=U1.s(O_ktok + h * 128, 128), scalar1=decs.s(8 + bl, 1),
                                                       scalar2=None, op0=mult), reads=[B_seg[2], B_const], writes=[B_kms[jk]])
                T.op("pe", lambda e: e.matmul(ps_misc.s(0, 256), kms[jk].s(0, 128), v_h, start=True, stop=True),
                     reads=[B_kms[jk], B_seg[3]], writes=[B_misc])
                jo = nxt("of")
                T.op("dve", lambda e: e.tensor_tensor(out=of32[jo].s(0, 256), in0=ps_misc.s(0, 256), in1=xin[jx].s(0, 256), op=add),
                     reads=[B_misc, B_xin[jx]], writes=[B_of[jo]])
                T.op("dve", lambda e: e.tensor_scalar(out=of32[jo].s(0, 256), in0=of32[jo].s(0, 256), scalar1=decs.s(4 + h, 1), scalar2=None, op0=mult),
                     reads=[B_of[jo], B_const], writes=[B_of[jo]])
                T.dma("sp", lambda e: e.dma_start(out=dap(rss_d, (bl * 4 + h) * 128 * 256, [[256, 128], [1, 256]]), in_=of32[jo].s(0, 256)),
                      reads=[B_of[jo]], writes=[Buf()])
            gn_tail(U1.s(O_on + h * 256, 256))

        def sB(i, wb, Bw):
            if i < 2:
                pm, Bpm = inproj_tile(wb, Bw, 0)
                jf = nxt("kf")
                T.op("act", lambda e: e.activation(out=kf[jf].s(0, 512), in_=pm.s(0, 512), func=AF.Silu, scale=rstd.s(0, 1)),
                     reads=[Bpm, B_rstd], writes=[B_kf[jf]])
                o_ap = U1.s(O_on + i * 512, 512)
                T.op("dve", lambda e: e.tensor_tensor(out=o_ap, in0=o_ap, in1=kf[jf].s(0, 512), op=mult), reads=[B_kf[jf], B_seg[4]], writes=[B_seg[4]])
                if i == 1:
                    transposes_to(U1, B_seg[4], O_on, 8, U1.ap(O_gretT, [[512, 8], [1, 128]]), [B_seg[0], B_seg[1]])
            elif i in (2, 4):
                pm, Bpm = inproj_tile(wb, Bw, 0)
                T.op("act", lambda e: e.activation(out=sigb.s(0, 512), in_=pm.s(0, 512), func=AF.Sigmoid, scale=rstd.s(0, 1)),
                     reads=[Bpm, B_rstd], writes=[B_sig])
            else:
                j = (i - 3) // 2
                pm, Bpm = inproj_tile(wb, Bw, 0, 8, lhs=lambda kt: U1.s(O_gretT + kt * 512, 128), Blhs=[B_seg[0], B_seg[1]])
                T.op("dve", lambda e: e.tensor_tensor(out=A.s(j * 512, 512), in0=pm.s(0, 512), in1=sigb.s(0, 512), op=mult),
                     reads=[Bpm, B_sig], writes=[B_A])
        run_loads([(wbf_d, 4, 8), (wbf_d, 5, 8), (wbf_d, 16, 8), (wprb_d, 0, 8), (wbf_d, 17, 8), (wprb_d, 1, 8)], sB)

        O_qn, O_kn, O_vn, O_sg = O_aqT, O_aqT + 1536, O_aqT + 3072, O_aqT + 4608

        def sC(i, wb, Bw):
            c = 6 + i
            pm, Bpm = inproj_tile(wb, Bw, 0)
            if c < 9:
                jb, _ = head_rms(pm, Bpm, 0, gq_tab, False)
                T.op("dve", lambda e: e.tensor_copy(out=U1.s(O_qn + (c - 6) * 512, 512), in_=kb[jb].s(0, 512)), reads=[B_kb[jb]], writes=segs012)
            elif c < 12:
                jb, jo = head_rms(pm, Bpm, 0, gk_tab, True)
                kv_store(jo, kvs_d[c - 9], 0, 0)
                T.op("dve", lambda e: e.tensor_copy(out=U1.s(O_kn + (c - 9) * 512, 512), in_=kb[jb].s(0, 512)), reads=[B_kb[jb]], writes=segs012)
            elif c < 15:
                jo = v_epilogue(pm, Bpm, 0, U1.s(O_vn + (c - 12) * 512, 512), segs012, True)
                kv_store(jo, kvs_d[c - 12], 0, 1)
            else:
                T.op("act", lambda e: e.activation(out=U1.s(O_sg, 512), in_=pm.s(0, 512), func=AF.Silu, scale=rstd.s(0, 1)),
                     reads=[Bpm, B_rstd], writes=segs012)
        run_loads([(wbf_d, c, 8) for c in range(6, 16)], sC)

        B_att = Buf("att_scr")
        sm2 = sb("sm2", 64)
        B_sm2 = [Buf("sm2a"), Buf("sm2b")]
        B_sq2 = B_sqh
        B_rot2x = [Buf("rot2a"), Buf("rot2b")]
        T.op("dve", lambda e: e.memset(sm2.s(0, 64), 0.0), reads=[B_rot], writes=B_sm2 + B_rot2x)
        un = [0]
        nrows = [128, 512, 2048]
        dil = [1, 4, 16]
        pg = [(p_, g_) for p_ in range(16) for g_ in range(3)]

        def cache_load(p_, g_):
            jx_ = nxt("xin")
            base_ = ((p_ // 4) * nrows[g_] + ((p_ % 4) if g_ > 0 else 0)) * 1024
            T.dma("sp", lambda e: e.dma_start(out=xin[jx_].s(0, 1024), in_=dap(cch_d[g_], base_, [[dil[g_] * 1024, 128], [1, 1024]])),
                  writes=[B_xin[jx_]])
            return jx_
        pend = {pg[0]: cache_load(*pg[0])}
        for p in range(16):
            bl, i = p // 4, p % 4
            first = True
            for g in range(3):
                jx_c = pend.pop((p, g))
                if p * 3 + g + 1 < len(pg):
                    pend[pg[p * 3 + g + 1]] = cache_load(*pg[p * 3 + g + 1])
                m = nxt("mm")
                T.op("pe", lambda e: e.matmul(ps_mm[m].s(0, 512), ident_b.ap(p, [[0, 128]]), U1.s(O_qn + g * 512, 512), start=True, stop=True),
                     reads=[B_const] + segs012, writes=[B_mm[m]])
                for tile in range(2):
                    if tile == 0:
                        jx = jx_c
                        K_ap, V_ap, BK = xin[jx].ap(0, [[64, 8], [1, 64]]), xin[jx].ap(512, [[64, 8], [1, 64]]), [B_xin[jx]]
                        tcol = (i if g == 0 else 3 + g) * 8
                    else:
                        K_ap, V_ap, BK = U1.ap(O_kn + g * 512, [[64, 8], [1, 64]]), U1.ap(O_vn + g * 512, [[64, 8], [1, 64]]), segs012
                        tcol = (6 + g * 16 + p) * 8
                    u2 = un[0] % 2
                    un[0] += 1
                    so, lo = u2 * 512, u2 * 16
                    T.op("dve", lambda e: e.tensor_tensor(out=sq.ap(so, [[64, 8], [1, 64]]), in0=K_ap, in1=ps_mm[m].ap(0, [[64, 8], [1, 64]]), op=mult),
                         reads=BK + [B_mm[m]], writes=[B_sq2[u2]])
                    T.op("dve", lambda e: e.reduce_sum(out=sm2.s(lo, 8), in_=sq.ap(so, [[64, 8], [1, 64]]), axis=AX.X), reads=[B_sq2[u2]], writes=[B_sm2[u2]])
                    T.op("act", lambda e: e.activation(out=sm2.s(lo + 8, 8), in_=sm2.s(lo, 8), func=AF.Exp), reads=[B_sm2[u2]], writes=[B_sm2[u2]])
                    T.op("dve", lambda e: e.tensor_tensor(out=sm2.s(lo + 8, 8), in0=sm2.s(lo + 8, 8), in1=EBs.s(tcol, 8), op=mult),
                         reads=[B_sm2[u2], B_const], writes=[B_sm2[u2]])
                    T.op("dve", lambda e: e.tensor_tensor(out=rot.ap(so, [[64, 8], [1, 64]]), in0=V_ap, in1=sm2.ap(lo + 8, [[1, 8], [0, 64]]), op=mult),
                         reads=BK + [B_sm2[u2]], writes=[B_rot2x[u2]])
                    T.op("pe", lambda e: e.matmul(ps_num.s(0, 512, n=1), onesf.s(0, 1), rot.s(so, 512), start=first, stop=False, skip_group_check=True),
                         reads=[B_rot2x[u2], B_const], writes=[B_num])
                    T.op("pe", lambda e: e.matmul(ps_den.s(0, 8, n=1), onesf.s(0, 1), sm2.s(lo + 8, 8), start=first, stop=False, skip_group_check=True),
                         reads=[B_sm2[u2], B_const], writes=[B_den])
                    first = False
            T.op("dve", lambda e: e.reciprocal(out=small.s(56, 8, n=1), in_=ps_den.s(0, 8, n=1)), reads=[B_den], writes=[B_small])
            jf = nxt("kf")
            T.op("dve", lambda e: e.tensor_tensor(out=kf[jf].ap(0, [[64, 8], [1, 64]], n=1), in0=ps_num.ap(0, [[64, 8], [1, 64]], n=1),
                                                  in1=small.ap(56, [[1, 8], [0, 64]], n=1), op=mult), reads=[B_num, B_small], writes=[B_kf[jf]])
            T.dma("sp", lambda e: e.dma_start(out=dap(att_scr_d, p * 512, [[512, 1], [1, 512]]), in_=kf[jf].s(0, 512, n=1)),
                  reads=[B_kf[jf]], writes=[B_att])
        jf = nxt("kf")
        T.dma("sp", lambda e: e.dma_start(out=kf[jf].s(0, 512, n=16), in_=dap(att_scr_d, 0, [[512, 16], [1, 512]])), reads=[B_att], writes=[B_kf[jf]])
        jb = nxt("kb", 6)
        T.op("dve", lambda e: e.memset(kb[jb].s(0, 512), 0.0), writes=[B_kb[jb]])
        T.op("dve", lambda e: e.tensor_tensor(out=kb[jb].s(0, 512, n=16), in0=kf[jf].s(0, 512, n=16), in1=U1.s(O_sg, 512, n=16), op=mult),
             reads=[B_kf[jf]] + segs012, writes=[B_kb[jb]])
        transposes_to(kb[jb], B_kb[jb], 0, 4, U1.ap(O_gatT, [[512, 4], [1, 128]]), [B_seg[3]])

        def sD(i, wb, Bw):
            if i in (0, 2):
                pm, Bpm = inproj_tile(wb, Bw, 0)
                T.op("act", lambda e: e.activation(out=sigb.s(0, 512), in_=pm.s(0, 512), func=AF.Sigmoid, scale=rstd.s(0, 1)),
                     reads=[Bpm, B_rstd], writes=[B_sig])
            elif i in (1, 3):
                j = (i - 1) // 2
                pm, Bpm = inproj_tile(wb, Bw, 0, 4, lhs=lambda kt: U1.s(O_gatT + kt * 512, 128), Blhs=[B_seg[3]])
                jf = nxt("kf")
                T.op("dve", lambda e: e.tensor_tensor(out=kf[jf].s(0, 512), in0=pm.s(0, 512), in1=sigb.s(0, 512), op=mult),
                     reads=[Bpm, B_sig], writes=[B_kf[jf]])
                T.op("dve", lambda e: e.tensor_tensor(out=U1.s(O_mrg + j * 512, 512), in0=kf[jf].s(0, 512), in1=A.s(j * 512, 512), op=add),
                     reads=[B_kf[jf], B_A], writes=[B_seg[4]])
                if i == 3:
                    transposes_to(U1, B_seg[4], O_mrg, 8, U1.ap(O_mT, [[512, 8], [1, 128]]), [B_seg[0], B_seg[1]])
            else:
                j = i - 4
                pm, Bpm = inproj_tile(wb, Bw, 0, 8, lhs=lambda kt: U1.s(O_mT + kt * 512, 128), Blhs=[B_seg[0], B_seg[1]])
                jx = nxt("xin")
                T.dma("sp", lambda e: e.dma_start(out=xin[jx].s(0, 512), in_=dap(xs_d, j * 512, [[D, 128], [1, 512]])), writes=[B_xin[jx]])
                jo = nxt("of")
                T.op("dve", lambda e: e.tensor_tensor(out=of32[jo].s(0, 512), in0=pm.s(0, 512), in1=xin[jx].s(0, 512), op=add),
                     reads=[Bpm, B_xin[jx]], writes=[B_of[jo]])
                T.dma("sp", lambda e: e.dma_start(out=dap(ys_d, j * 512, [[D, 16], [1, 512]]), in_=of32[jo].s(0, 512, n=16)),
                      reads=[B_of[jo]], writes=[Buf()])
        run_loads([(wbf_d, 18, 8), (wpab_d, 0, 4), (wbf_d, 19, 8), (wpab_d, 1, 4), (wob_d, 0, 8), (wob_d, 1, 8)], sD)

    if STAGE >= 2:
        T.dma(QST, lambda e: e.dma_start(out=dap(sfin_d, 0, [[256, 128], [128 * 256, 4], [1, 256]]), in_=S.ap(0, [[256, 4], [1, 256]])),
              reads=B_S, writes=[Buf()])

    if "pad" in DBG:
        for i in range(int(os.environ.get("MK_PAD", "2000"))):
            if "padnoinc" in DBG and i > 0:
                nc.tensor.matmul(ps_st.s(0, 32), ident_b.s(0, 128), ident_b.s(0, 32), start=True, stop=True, skip_group_check=True)
                continue
            T.op("pe", lambda e: e.matmul(ps_st.s(0, 32), ident_b.s(0, 128), ident_b.s(0, 32), start=True, stop=True, skip_group_check=True),
                 reads=[B_const], writes=[B_st[0]])
    T.finish()
    es.close()
    print("[kernel] sbuf bytes/partition (persistent): %d ; waits=%d ; counts=%s ; waits_by_eng=%s ; dmas=%d" % (
        sbuf_used[0], T.n_wait, {k: v.count for k, v in T.e.items()}, {k: getattr(v, "nw", 0) for k, v in T.e.items()}, T.di))
    return nc


def t5_bucket_np(dist):
    max_exact = 16
    d = np.maximum(dist.astype(np.float32), np.float32(1.0))
    large = max_exact + (np.log(d / np.float32(max_exact)) / np.float32(np.log(2048 / 16)) * np.float32(16)).astype(np.int32)
    large = np.minimum(large, 31)
    return np.where(dist < max_exact, dist, large)


def host_constants():
    c = {}
    c["ident"] = np.eye(128, dtype=np.float32)
    kq = np.arange(128)
    c["cmask"] = (kq[None, :] >= kq[:, None]).astype(np.float32)
    j = np.arange(128)[:, None]
    cc = np.arange(256)[None, :]
    m = np.where(cc < 128, 128 + cc - j, cc - 128 - j)
    valid = (m >= 0) & (m <= 128)
    c["tmask"] = valid.astype(np.float32)
    c["m_idx"] = np.clip(m, 0, 128)
    H = 4
    log_g = np.log1p(-(2.0 ** (-5.0 - np.arange(H, dtype=np.float64))))
    i = np.arange(128, dtype=np.float64)[:, None]
    qdec = np.exp((i + 1.0) * log_g[None, :])
    kdec = np.exp(-(i + 1.0) * log_g[None, :]) * (128 ** -0.5)
    cd = np.exp(128.0 * log_g)[None, :].repeat(128, 0)
    c["dec"] = np.concatenate([qdec, kdec, cd, kdec * cd], axis=1).astype(np.float32)
    c["log_g"] = log_g
    return c


def kernel(x_prompt, x_sample, cache_kv_w128, cache_kv_w512, cache_kv_w2048, state_retention,
           w_norm, w_in, q_norm, k_norm, rel_bias, ret_norm, w_proj_ret, w_proj_att, w_out):
    f32 = np.float32
    x_prompt = np.asarray(x_prompt, f32)
    hc = host_constants()
    nc = build_program()

    w_in0 = np.ascontiguousarray(np.asarray(w_in, f32)[0])
    gT = np.ascontiguousarray(np.asarray(w_norm, f32)[0].reshape(8, 128).T)
    rnT = np.ascontiguousarray(np.asarray(ret_norm, f32)[0].reshape(8, 128).T)
    gq_tab = np.ascontiguousarray(np.broadcast_to(np.asarray(q_norm, f32)[0][None, :], (128, 64)))
    gk_tab = np.ascontiguousarray(np.broadcast_to(np.asarray(k_norm, f32)[0][None, :], (128, 64)))
    rb = np.asarray(rel_bias, f32)
    dils = [1, 4, 16]
    bt = np.zeros((128, 24, 256), f32)
    for g in range(3):
        bidx = t5_bucket_np((dils[g] * hc["m_idx"]).astype(np.int32))
        for h in range(8):
            bt[:, g * 8 + h, :] = rb[bidx, g * 8 + h]
    bt = bt.reshape(128, 24 * 256)
    half = 32
    inv = (np.float32(10000.0) ** (-np.arange(half * 2, dtype=f32) / np.float32(half * 2)))
    xs_all = np.asarray(x_sample, f32).reshape(128, D)
    sr = np.asarray(state_retention, f32)[0]
    lg = hc["log_g"]
    i_of = (np.arange(128) % 4).astype(np.float64)[:, None]
    kdec_s = np.exp(-(i_of + 1.0) * lg[None, :]) * (128 ** -0.5)
    g4 = np.exp(4.0 * lg)[None, :].repeat(128, 0)
    bmask = np.zeros((128, 4))
    for bl in range(4):
        bmask[4 * bl:4 * bl + 4, bl] = 1.0
    qdec_s = np.exp((i_of + 1.0) * lg[None, :])
    dec_s = np.concatenate([kdec_s, g4, bmask, qdec_s], axis=1).astype(f32)
    tk = np.arange(128)
    smask = ((tk[:, None] // 4 == tk[None, :] // 4) & (tk[None, :] % 4 >= tk[:, None] % 4)).astype(f32)
    colmask = np.zeros((128, 4, 128), f32)
    for bl in range(4):
        colmask[:, bl, 4 * bl:4 * bl + 4] = 1.0
    colmask = colmask.reshape(128, 512)
    bts = np.zeros((128, 54, 8), f32)
    vms = np.zeros((128, 54, 8), f32)
    kk = np.arange(128)
    for i in range(4):
        mm = 128 + i - kk
        ok = mm <= 128
        bts[:, i, :] = rb[t5_bucket_np(np.clip(mm, 0, 128).astype(np.int32)), 0:8]
        vms[:, i, :] = ok[:, None]
    for g in (1, 2):
        mm = 128 - kk
        bts[:, 3 + g, :] = rb[t5_bucket_np((dils[g] * mm).astype(np.int32)), g * 8:(g + 1) * 8]
        vms[:, 3 + g, :] = 1.0
    for g in range(3):
        for p in range(16):
            bl, i = p // 4, p % 4
            for k in range(16):
                bk, j = k // 4, k % 4
                if bk != bl:
                    continue
                if (g == 0 and j <= i) or (g > 0 and j == i):
                    m_ = i - j
                    bts[k, 6 + g * 16 + p, :] = rb[int(t5_bucket_np(np.array([dils[g] * m_], np.int32))[0]), g * 8:(g + 1) * 8]
                    vms[k, 6 + g * 16 + p, :] = 1.0
    bts = bts.reshape(128, 432)
    vms = vms.reshape(128, 432)
    caches = [np.asarray(cache_kv_w128, f32)[0], np.asarray(cache_kv_w512, f32)[0], np.asarray(cache_kv_w2048, f32)[0]]
    pos_s = (16384 + (np.arange(128) % 4)).astype(f32)
    ang_s = pos_s[:, None] * inv[None, :]
    in_maps = []
    for c in range(NCORES):
        b, p = c // 4, c % 4
        own = x_prompt[b, p * OWN:(p + 1) * OWN]
        halo = x_prompt[b, (p - 1) * OWN:p * OWN] if p > 0 else np.zeros((OWN, D), f32)
        pos = np.maximum((p - 3) * OWN + np.arange(4 * OWN), 0).astype(f32)
        ang = pos[:, None] * inv[None, :]
        xpp = np.zeros((3 * OWN, D), f32)
        if p > 0:
            xpp[(3 - p) * OWN:] = x_prompt[b, :p * OWN]
        in_maps.append({
            "x_h": np.ascontiguousarray(halo), "x_o": np.ascontiguousarray(own),
            "w_in": w_in0,
            "w_proj_ret": np.ascontiguousarray(np.asarray(w_proj_ret, f32)[0]),
            "w_proj_att": np.ascontiguousarray(np.asarray(w_proj_att, f32)[0]),
            "w_out": np.ascontiguousarray(np.asarray(w_out, f32)[0]),
            "gT": gT, "rnT": rnT, "gq_tab": gq_tab, "gk_tab": gk_tab,
            "bt": bt, "tmask": hc["tmask"],
            "cos": np.cos(ang).astype(f32), "sin": np.sin(ang).astype(f32),
            "dec": hc["dec"], "cmask": hc["cmask"], "ident": hc["ident"],
            "hv": np.full((128, 64), 1.0 if p > 0 else 0.0, f32),
            "x_pp": xpp,
            "x_s": np.ascontiguousarray(np.roll(xs_all, -16 * c, axis=0)),
            "cos_s": np.cos(ang_s).astype(f32), "sin_s": np.sin(ang_s).astype(f32), "dec_s": dec_s,
            "s0": np.ascontiguousarray(sr[4 * c:4 * c + 4].reshape(16, 128, 256)),
            "smask": smask, "colmask": colmask, "bts": bts, "vms": vms,
            "c128": np.ascontiguousarray(caches[0][4 * c:4 * c + 4].reshape(4 * 128, 2, 512)),
            "c512": np.ascontiguousarray(caches[1][4 * c:4 * c + 4].reshape(4 * 512, 2, 512)),
            "c2048": np.ascontiguousarray(caches[2][4 * c:4 * c + 4].reshape(4 * 2048, 2, 512)),
        })
    res = run_bass_kernel_spmd(nc, in_maps, core_ids=list(range(NCORES)))
    R = res.results
    y_p = np.stack([np.concatenate([R[b * 4 + p]["y"] for p in range(4)], 0) for b in range(2)], 0)
    rs_p = np.stack([R[b * 4 + 3]["sfin"].reshape(4, 128, 256) for b in range(2)], 0)[None]
    kv128 = np.stack([R[b * 4 + 3]["kv128"].reshape(128, 2, 8, 64) for b in range(2)], 0)[None]
    kv512 = np.stack([R[b * 4 + 3]["kv512"].reshape(512, 2, 8, 64) for b in range(2)], 0)[None]
    kv2048 = np.stack([R[b * 4 + 3]["kv2048"].reshape(2048, 2, 8, 64) for b in range(2)], 0)[None]
    y_s = np.stack([R[c]["ys"].reshape(4, 4, 1024) for c in range(NCORES)], 0).reshape(32, 4, 1024)
    rs_s = np.stack([R[c]["rss"].reshape(4, 4, 128, 256) for c in range(NCORES)], 0).reshape(1, 32, 4, 128, 256)
    kvs = [R[0]["kvs%d" % g].reshape(1, 32, 4, 2, 8, 64) for g in range(3)]
    return (y_p, y_s, rs_p, rs_s, kv128, kv512, kv2048, kvs[0], kvs[1], kvs[2])
```
